# Optimizing a Trainium2 kernel written in Bass

```python
import math
import jax
import jax.numpy as jnp
from jax import lax
import numpy as np

D_MODEL = 1024
BATCH = 8
SEQ = 4096
DEPTH = 2

D_MIX = D_MODEL
N_MIXERS = 4
GROUP_W = D_MIX // N_MIXERS
HEAD_DIM = 64
A_HEADS = GROUP_W // HEAD_DIM
A_KV_HEADS = A_HEADS // 2
C_HEADS = GROUP_W // HEAD_DIM
C_SUB = HEAD_DIM // 2
HY_EMB = 33
HY_BANDS = (HY_EMB - 1) // 2
HY_FFN = 64
HY_SHIFT = 0.05
HY_FAST = 0.3
HY_SLOW = 1.5
HY_TARGET = 1e-2
CONV_W = 3
GRID_W = 64
ROPE_THETA = 10000.0
Q_BLOCK = 128
EPS = 1e-6

SPLIT_SIZES = (
    A_HEADS * HEAD_DIM, A_KV_HEADS * HEAD_DIM, A_KV_HEADS * HEAD_DIM, GROUP_W,
    3 * GROUP_W, GROUP_W,
    2 * C_HEADS * C_SUB, 2 * C_HEADS * C_SUB, C_HEADS * HEAD_DIM, GROUP_W,
    3 * GROUP_W, GROUP_W,
)
D_IN = sum(SPLIT_SIZES)
SPLIT_POINTS = tuple(sum(SPLIT_SIZES[:i + 1]) for i in range(len(SPLIT_SIZES) - 1))

kernel_name = "hymba_style_bidir_hybrid_encoder"


def rmsnorm(x, g):
    xf = x.astype(jnp.float32)
    y = xf * lax.rsqrt(jnp.mean(xf * xf, axis=-1, keepdims=True) + EPS)
    return (y * g.astype(jnp.float32)).astype(x.dtype)


def rope_cos_sin(pos, dim):
    inv = ROPE_THETA ** (-jnp.arange(0, dim, 2, dtype=jnp.float32) / dim)
    ang = pos.astype(jnp.float32)[:, None] * inv[None, :]
    return jnp.cos(ang), jnp.sin(ang)


def apply_rope(x, cs):
    cos, sin = cs
    xf = x.astype(jnp.float32)
    x1, x2 = jnp.split(xf, 2, axis=-1)
    c, s = cos[:, None, :], sin[:, None, :]
    return jnp.concatenate([x1 * c - x2 * s, x2 * c + x1 * s], axis=-1).astype(x.dtype)


def apply_axial_rope(x, cs_row, cs_col):
    xr, xc = jnp.split(x, 2, axis=-1)
    return jnp.concatenate([apply_rope(xr, cs_row), apply_rope(xc, cs_col)], axis=-1)


def dwconv3(x, w, b=None):
    xp = jnp.pad(x, ((0, 0), (1, 1), (0, 0)))
    y = xp[:, :-2] * w[0] + xp[:, 1:-1] * w[1] + xp[:, 2:] * w[2]
    return y if b is None else y + b


def to_blocks(q):
    b, s = q.shape[:2]
    q = q.reshape((b, s // Q_BLOCK, Q_BLOCK) + q.shape[2:])
    return jnp.moveaxis(q, 1, 0)


def from_blocks(o):
    o = jnp.moveaxis(o, 0, 1)
    return o.reshape((o.shape[0], o.shape[1] * o.shape[2]) + o.shape[3:])


def gqa_attention(q, k, v):
    b, s, hq, d = q.shape
    hkv = k.shape[2]
    qg = q.reshape(b, s, hkv, hq // hkv, d)
    scale = d ** -0.5

    def block(qb):
        sc = jnp.einsum('bqhgd,bshd->bhgqs', qb, k, preferred_element_type=jnp.float32) * scale
        p = jax.nn.softmax(sc, axis=-1).astype(v.dtype)
        return jnp.einsum('bhgqs,bshd->bqhgd', p, v)

    o = from_blocks(lax.map(block, to_blocks(qg)))
    return o.reshape(b, s, hq * d)


def diff_attention(q, k, v, lam):
    scale = q.shape[-1] ** -0.5

    def block(qb):
        sc = jnp.einsum('bqhcd,bshcd->bhcqs', qb, k, preferred_element_type=jnp.float32) * scale
        p = jax.nn.softmax(sc, axis=-1)
        w = (p[:, :, 0] - lam * p[:, :, 1]).astype(v.dtype)
        return jnp.einsum('bhqs,bshd->bqhd', w, v)

    return from_blocks(lax.map(block, to_blocks(q)))


def hyena_filter(L, w1, b1, freq, w2, b2, w3):
    f32 = jnp.float32
    t = jnp.linspace(0.0, 1.0, L, dtype=f32)[:, None]
    w = 2.0 * math.pi * jnp.arange(L, dtype=f32)[:, None] / L
    f = jnp.linspace(1e-4, HY_BANDS - 1, HY_BANDS, dtype=f32)[None, :]
    emb = jnp.concatenate([t, jnp.cos(f * w), -jnp.sin(f * w)], axis=-1)
    fr = freq.astype(f32)
    h = jnp.sin(fr * (emb @ w1.astype(f32) + b1.astype(f32)))
    h = jnp.sin(fr * (h @ w2.astype(f32) + b2.astype(f32)))
    h = h @ w3.astype(f32)
    max_decay = math.log(HY_TARGET) / HY_FAST
    min_decay = math.log(HY_TARGET) / HY_SLOW
    deltas = jnp.linspace(min_decay, max_decay, GROUP_W, dtype=f32)
    window = jnp.exp(-t * jnp.abs(deltas)[None, :]) + HY_SHIFT
    h_fwd = h[:, :GROUP_W] * window
    h_bwd = h[:, GROUP_W:] * window
    kern = jnp.concatenate([h_fwd, jnp.zeros((1, GROUP_W), f32), h_bwd[:0:-1]], axis=0)
    return kern / jnp.sum(jnp.abs(kern), axis=0, keepdims=True)


def bidir_fftconv(z, kern, bias):
    L = z.shape[1]
    n = 2 * L
    zf32 = z.astype(jnp.float32)
    zf = jnp.fft.rfft(zf32, n=n, axis=1)
    kf = jnp.fft.rfft(kern, n=n, axis=0)
    y = jnp.fft.irfft(zf * kf[None], n=n, axis=1)[:, :L]
    return (y + bias.astype(jnp.float32) * zf32).astype(z.dtype)


def hybrid_layer(x, c, layer_idx, cs_row, cs_col, cs_seq, norm_g, w_ada, b_ada, w_in, w_out,
                 a_qn, a_kn, hy_conv_w, hy_conv_b, hy_w1, hy_b1, hy_freq, hy_w2, hy_b2, hy_w3,
                 hy_bias, c_qn, c_kn, lam_q1, lam_k1, lam_q2, lam_k2, c_subln, sc_conv_w):
    b, s, _ = x.shape
    mod = jax.nn.silu(c) @ w_ada + b_ada
    shift, scale, gate = jnp.split(mod, 3, axis=-1)
    h = rmsnorm(x, norm_g) * (1.0 + scale[:, None, :]) + shift[:, None, :]
    proj = h @ w_in
    (a_q, a_k, a_v, a_g, b_p, b_g, c_q, c_k, c_v, c_g, d_p, d_g) = jnp.split(proj, SPLIT_POINTS, axis=-1)

    qa = apply_axial_rope(rmsnorm(a_q.reshape(b, s, A_HEADS, HEAD_DIM), a_qn), cs_row, cs_col)
    ka = apply_axial_rope(rmsnorm(a_k.reshape(b, s, A_KV_HEADS, HEAD_DIM), a_kn), cs_row, cs_col)
    va = a_v.reshape(b, s, A_KV_HEADS, HEAD_DIM)
    y_a = jax.nn.silu(a_g) * gqa_attention(qa, ka, va)

    x0, x1, vb = jnp.split(dwconv3(b_p, hy_conv_w, hy_conv_b), 3, axis=-1)
    kern = hyena_filter(s, hy_w1, hy_b1, hy_freq, hy_w2, hy_b2, hy_w3)
    y_b = jax.nn.silu(b_g) * (x0 * bidir_fftconv(x1 * vb, kern, hy_bias))

    qc = apply_rope(rmsnorm(c_q.reshape(b, s, 2 * C_HEADS, C_SUB), c_qn), cs_seq).reshape(b, s, C_HEADS, 2, C_SUB)
    kc = apply_rope(rmsnorm(c_k.reshape(b, s, 2 * C_HEADS, C_SUB), c_kn), cs_seq).reshape(b, s, C_HEADS, 2, C_SUB)
    vc = c_v.reshape(b, s, C_HEADS, HEAD_DIM)
    lambda_init = 0.8 - 0.6 * math.exp(-0.3 * layer_idx)
    lam = (jnp.exp(jnp.sum(lam_q1.astype(jnp.float32) * lam_k1.astype(jnp.float32)))
           - jnp.exp(jnp.sum(lam_q2.astype(jnp.float32) * lam_k2.astype(jnp.float32)))
           + lambda_init)
    oc = rmsnorm(diff_attention(qc, kc, vc, lam), c_subln) * (1.0 - lambda_init)
    y_c = jax.nn.silu(c_g) * oc.reshape(b, s, C_HEADS * HEAD_DIM)

    bg, cg, xd = jnp.split(d_p, 3, axis=-1)
    y_d = jax.nn.silu(d_g) * (bg * dwconv3(cg * xd, sc_conv_w))

    out = jnp.concatenate([y_a, y_b, y_c, y_d], axis=-1) @ w_out
    return x + gate[:, None, :] * out


def setup_inputs(seed: int = 0) -> dict:
    key = jax.random.key(seed)
    ks = jax.random.split(key, 26)

    def nrm(k, shape, scale):
        return jax.random.normal(k, shape, jnp.float32) * scale

    def gain(k, shape, noise=0.02):
        return 1.0 + noise * jax.random.normal(k, shape, jnp.float32)

    L = DEPTH
    return {
        'x': nrm(ks[0], (BATCH, SEQ, D_MODEL), 1.0),
        'c': nrm(ks[1], (BATCH, D_MODEL), 1.0),
        'norm_g': gain(ks[2], (L, D_MODEL)),
        'w_ada': nrm(ks[3], (L, D_MODEL, 3 * D_MODEL), 0.5 * D_MODEL ** -0.5),
        'b_ada': nrm(ks[4], (L, 3 * D_MODEL), 0.01),
        'w_in': nrm(ks[5], (L, D_MODEL, D_IN), D_MODEL ** -0.5),
        'w_out': nrm(ks[6], (L, D_MIX, D_MODEL), D_MIX ** -0.5),
        'a_qn': gain(ks[7], (L, HEAD_DIM)),
        'a_kn': gain(ks[8], (L, HEAD_DIM)),
        'hy_conv_w': nrm(ks[9], (L, CONV_W, 3 * GROUP_W), CONV_W ** -0.5),
        'hy_conv_b': nrm(ks[10], (L, 3 * GROUP_W), 0.01),
        'hy_w1': nrm(ks[11], (L, HY_EMB, HY_FFN), HY_EMB ** -0.5),
        'hy_b1': nrm(ks[12], (L, HY_FFN), 0.1),
        'hy_freq': gain(ks[13], (L, HY_FFN), 0.1),
        'hy_w2': nrm(ks[14], (L, HY_FFN, HY_FFN), HY_FFN ** -0.5),
        'hy_b2': nrm(ks[15], (L, HY_FFN), 0.1),
        'hy_w3': nrm(ks[16], (L, HY_FFN, 2 * GROUP_W), HY_FFN ** -0.5),
        'hy_bias': nrm(ks[17], (L, GROUP_W), 1.0),
        'c_qn': gain(ks[18], (L, C_SUB)),
        'c_kn': gain(ks[19], (L, C_SUB)),
        'lam_q1': nrm(ks[20], (L, C_SUB), 0.1),
        'lam_k1': nrm(ks[21], (L, C_SUB), 0.1),
        'lam_q2': nrm(ks[22], (L, C_SUB), 0.1),
        'lam_k2': nrm(ks[23], (L, C_SUB), 0.1),
        'c_subln': gain(ks[24], (L, HEAD_DIM)),
        'sc_conv_w': nrm(ks[25], (L, CONV_W, GROUP_W), CONV_W ** -0.5),
    }


def reference(x, c, norm_g, w_ada, b_ada, w_in, w_out, a_qn, a_kn, hy_conv_w, hy_conv_b,
              hy_w1, hy_b1, hy_freq, hy_w2, hy_b2, hy_w3, hy_bias, c_qn, c_kn,
              lam_q1, lam_k1, lam_q2, lam_k2, c_subln, sc_conv_w):
    s = x.shape[1]
    rows = s // GRID_W
    t = jnp.arange(s, dtype=jnp.int32)
    row = jnp.repeat(jnp.arange(rows, dtype=jnp.int32), GRID_W)
    col = jnp.tile(jnp.arange(GRID_W, dtype=jnp.int32), rows)
    cs_row = rope_cos_sin(row, HEAD_DIM // 2)
    cs_col = rope_cos_sin(col, HEAD_DIM // 2)
    cs_seq = rope_cos_sin(t, C_SUB)
    for l in range(DEPTH):
        x = hybrid_layer(x, c, l, cs_row, cs_col, cs_seq, norm_g[l], w_ada[l], b_ada[l], w_in[l], w_out[l],
                         a_qn[l], a_kn[l], hy_conv_w[l], hy_conv_b[l], hy_w1[l], hy_b1[l], hy_freq[l],
                         hy_w2[l], hy_b2[l], hy_w3[l], hy_bias[l], c_qn[l], c_kn[l],
                         lam_q1[l], lam_k1[l], lam_q2[l], lam_k2[l], c_subln[l], sc_conv_w[l])
    return x
```

```python
import math
import contextlib
import numpy as np
import concourse.bass as bass
import concourse.mybir as mybir
from concourse.bass import AP
from concourse.bass_utils import run_bass_kernel_spmd

F32 = mybir.dt.float32
BF16 = mybir.dt.bfloat16
AF = mybir.ActivationFunctionType
ALU = mybir.AluOpType
AX = mybir.AxisListType

D_MODEL = 1024
D_IN = 3840
EPS = 1e-6
MAGIC = 12582912.0
TWO_PI = 2.0 * math.pi

O_AQ, O_AK, O_AV, O_AG = 0, 256, 384, 512
O_BP, O_BG = 768, 1536
O_CQ, O_CK, O_CV, O_CG = 1792, 2048, 2304, 2560
O_DP, O_DG = 2816, 3584


class Op:
    __slots__ = ("eng", "fn", "deps", "signal", "sigval", "is_dma", "dsem", "dval", "idx", "predma")

    def __init__(self, eng, fn, is_dma=False):
        self.eng = eng
        self.fn = fn
        self.deps = []
        self.signal = False
        self.sigval = None
        self.is_dma = is_dma
        self.dsem = None
        self.dval = None
        self.predma = None
        self.idx = None


ENGS = ("pe", "act", "dve", "pool", "sp")
QUEUES = ("sp", "act", "pool")


class Prog:
    NPOOL = 8

    def __init__(self, nc):
        self.nc = nc
        self.ops = {e: [] for e in ENGS}
        self.last_w = {}
        self.readers = {}
        self.dma_hist = {q: [] for q in QUEUES}
        self.bar_deps = []
        self.bar_gen = 0
        self.eng_gen = {e: 0 for e in ENGS}

    def _add_dep(self, op, dep):
        if dep is None or dep is op:
            return
        if dep.eng == "pe" and op.eng == "pe" and not dep.is_dma and not op.is_dma:
            return
        op.deps.append(dep)
        if not dep.is_dma:
            dep.signal = True

    def op(self, eng, fn, reads=(), writes=(), is_dma=False):
        o = Op(eng, fn, is_dma)
        bankr = [k for k in reads if isinstance(k, tuple) and k[0] == "bank"]
        if bankr:
            reads = [k for k in reads if k not in bankr]
            writes = list(writes) + [k for k in bankr if k not in writes]
        if self.eng_gen[eng] != self.bar_gen:
            self.eng_gen[eng] = self.bar_gen
            for d in self.bar_deps:
                if d.eng == eng and not d.is_dma and not is_dma:
                    continue
                self._add_dep(o, d)
        for k in reads:
            self._add_dep(o, self.last_w.get(k))
        for k in writes:
            self._add_dep(o, self.last_w.get(k))
            for r in self.readers.get(k, ()):
                if r.eng == eng and not r.is_dma and not is_dma:
                    continue
                self._add_dep(o, r)
        for k in reads:
            lst = self.readers.setdefault(k, [])
            if not is_dma:
                lst[:] = [r for r in lst if r.is_dma or r.eng != eng]
            lst.append(o)
        for k in writes:
            self.last_w[k] = o
            self.readers[k] = []
        if is_dma:
            hist = self.dma_hist[eng]
            j = len(hist)
            o.idx = j
            if j >= self.NPOOL:
                o.predma = hist[j - self.NPOOL]
            hist.append(o)
        self.ops[eng].append(o)
        return o

    def dma(self, q, out, in_, reads=(), writes=(), **kw):
        def fn(e, out=out, in_=in_, kw=kw):
            return e.dma_start(out=out, in_=in_, **kw)
        return self.op(q, fn, reads, writes, is_dma=True)

    def barrier(self):
        lasts = []
        for e in ENGS:
            for o in reversed(self.ops[e]):
                if not o.is_dma:
                    lasts.append(o)
                    break
        for q in QUEUES:
            lasts.extend(self.dma_hist[q][-self.NPOOL:])
        self.bar_deps = lasts
        self.bar_gen += 1

    def emit(self):
        nc = self.nc
        for e in ENGS:
            c = 0
            for o in self.ops[e]:
                if o.is_dma:
                    continue
                if o.signal:
                    c += 1
                    o.sigval = c
        with contextlib.ExitStack() as st:
            esem = {e: st.enter_context(nc.semaphore(f"s_{e}")) for e in ENGS}
            dsem = {q: [st.enter_context(nc.semaphore(f"d_{q}{i}")) for i in range(self.NPOOL)]
                    for q in QUEUES}
            for q in QUEUES:
                for o in self.dma_hist[q]:
                    o.dsem = dsem[q][o.idx % self.NPOOL]
                    o.dval = 16 * (o.idx // self.NPOOL + 1)
            block = st.enter_context(nc.Block())

            def run(ename, eng):
                seen = {}
                for o in self.ops[ename]:
                    waits = {}
                    deps = list(o.deps)
                    if o.predma is not None:
                        deps.append(o.predma)
                    for d in deps:
                        if d.is_dma:
                            s, v = d.dsem, d.dval
                        else:
                            s, v = esem[d.eng], d.sigval
                        key = id(s)
                        if seen.get(key, 0) >= v:
                            continue
                        if key not in waits or waits[key][1] < v:
                            waits[key] = (s, v)
                    for key, (s, v) in waits.items():
                        eng.wait_ge(s, v)
                        seen[key] = v
                    ins = o.fn(eng)
                    if o.is_dma:
                        ins.then_inc(o.dsem, 16)
                    elif o.signal:
                        ins.then_inc(esem[ename], 1)

            @block.sync
            def _(sync):
                run("sp", sync)

            @block.scalar
            def _(scalar):
                run("act", scalar)

            @block.vector
            def _(vector):
                run("dve", vector)

            @block.gpsimd
            def _(gpsimd):
                run("pool", gpsimd)

            @block.tensor
            def _(tensor):
                run("pe", tensor)


def cust(base, delta, dims):
    pd = list(base.ap[0])
    return AP(base.tensor, base.offset + delta, [pd] + [list(d) for d in dims])


class Arena:
    def __init__(self, nc, words):
        self.t = nc.alloc_sbuf_tensor("arena", [128, words], F32)
        self.words = words
        self.off = 0
        self.marks = []

    def alloc(self, shape, dt, parts=128):
        esz = 4 if dt == F32 else 2
        n = int(np.prod(shape))
        w = (n * esz + 3) // 4
        w = (w + 15) // 16 * 16
        assert self.off + w <= self.words, f"SBUF arena overflow: {self.off}+{w} > {self.words}"
        v = self.t[0:parts, self.off:self.off + (n * esz + 3) // 4]
        if dt != F32:
            v = v.bitcast(dt)
        self.off += w
        return v[:, 0:n]

    def mark(self):
        self.marks.append(self.off)

    def release(self):
        self.off = self.marks.pop()


def build(S, NL, debug=False, stages="MF1ACBO"):
    NT = S // 128
    NCH = S // 512
    L2 = 2 * S
    NF = L2 // 512
    W = L2 - 128
    CPB = 512 // NT
    assert CPB >= 1 and 256 % CPB == 0

    nc = bass.Bass("TRN2", target_bir_lowering=False)
    P = Prog(nc)
    scratch_kind = "ExternalOutput" if debug else "Internal"

    def din(name, shape, dt=F32):
        return nc.dram_tensor(name, list(shape), dt, kind="ExternalInput")

    def dscr(name, shape, dt=BF16):
        return nc.dram_tensor(name, list(shape), dt, kind=scratch_kind)

    x_in = din("x", [S, D_MODEL])
    cT_d = din("cT", [128, 8])
    normg_d = din("norm_g", [NL, 1, D_MODEL])
    wada_d = din("w_ada", [NL, D_MODEL, 3 * D_MODEL])
    bada_d = din("b_ada", [NL, 1, 3 * D_MODEL])
    win_d = din("w_in", [NL, D_MODEL, D_IN])
    wout_d = din("w_out", [NL, D_MODEL, D_MODEL])
    gnrow_d = din("gnrow", [NL, 1, 1792])
    hycw_d = din("hy_conv_wT", [NL, 128, 6, 3])
    hycb_d = din("hy_conv_bT", [NL, 128, 6])
    hyw1_d = din("hy_w1", [NL, 33, 64])
    hyb1_d = din("hy_b1T", [NL, 64, 1])
    hyfr_d = din("hy_freqT", [NL, 64, 1])
    hyw2_d = din("hy_w2", [NL, 64, 64])
    hyb2_d = din("hy_b2T", [NL, 64, 1])
    hyw3_d = din("hy_w3", [NL, 64, 512])
    hybias_d = din("hy_biasT", [NL, 128, 2])
    lam_d = din("lamrow", [NL, 1, 128])
    subln_d = din("c_sublnT", [NL, 64, 1])
    sccw_d = din("sc_conv_wT", [NL, 128, 2, 3])
    rope_d = din("rope_tab", [S, 192])
    emb_d = din("emb_tab", [33, L2])
    tpos_d = din("tpos_tab", [1, L2])
    ndel_d = din("negdelta", [128, 2])
    eye_d = din("eye_tab", [128, 128])
    jmat_d = din("jmat_tab", [128, 128])

    out_d = nc.dram_tensor("out", [S, D_MODEL], F32, kind="ExternalOutput")
    xres_d = dscr("xres", [S, D_MODEL], F32)
    gflt_d = dscr("gflt", [256, L2])
    qTA_d = dscr("qTA", [256, S])
    kTA_d = dscr("kTA", [256, S])
    vA_d = dscr("vA", [S, 130])
    qTC_d = dscr("qTC", [256, S])
    kTC_d = dscr("kTC", [256, S])
    vC_d = dscr("vC", [S, 260])
    sgA_d = dscr("sgA", [256, S])
    sgC_d = dscr("sgC", [256, S])
    xgB_d = dscr("xgB", [256, S])
    zT_d = dscr("zT", [256, S])
    yT_d = dscr("yT", [1024, S])

    ar = Arena(nc, 51000)
    dbanks = [nc.alloc_psum_tensor(f"dbank{i}", [128, 1024], F32) for i in range(4)]

    def bk(i):
        return dbanks[i // 2][:, (i % 2) * 512:(i % 2 + 1) * 512]

    def bkb(i):
        return bk(i).bitcast(BF16)

    def bk2(i):
        return dbanks[i // 2][:, :]

    ident = ar.alloc([128], BF16)
    jmat = ar.alloc([128], BF16)
    ones_f = ar.alloc([128], F32)
    ones_b = ar.alloc([128], BF16)
    A_bc = ar.alloc([1024], F32)
    SH_bc = ar.alloc([1024], F32)
    G_bc = ar.alloc([1024], F32)
    GN_bc = ar.alloc([1792], F32)
    sc_col = ar.alloc([8], F32)
    cT_sb = ar.alloc([8], F32)
    hycw = ar.alloc([18], F32)
    hycb = ar.alloc([6], F32)
    sccw = ar.alloc([6], F32)
    hybias = ar.alloc([2], F32)
    ndel = ar.alloc([2], F32)
    subg = ar.alloc([1], F32)
    neglam = ar.alloc([1], F32)
    ppos = ar.off

    _uq = [0]

    def uq(name):
        _uq[0] += 1
        return (name, _uq[0])

    def mm(out, lhsT, rhs, start, stop, reads, writes, **kw):
        return P.op("pe", lambda e: e.matmul(out, lhsT=lhsT, rhs=rhs, start=start, stop=stop, **kw), reads, writes)

    def tr(out, in_, reads, writes):
        return P.op("pe", lambda e: e.transpose(out=out, in_=in_, identity=ident), reads + ["ident"], writes)

    def act(out, in_, func, reads, writes, **kw):
        return P.op("act", lambda e: e.activation(out=out, in_=in_, func=func, **kw), reads, writes)

    def tt(eng, out, in0, in1, op, reads, writes):
        return P.op(eng, lambda e: e.tensor_tensor(out=out, in0=in0, in1=in1, op=op), reads, writes)

    def ts(eng, out, in0, s1, s2, op0, op1, reads, writes):
        return P.op(eng, lambda e: e.tensor_scalar(out=out, in0=in0, scalar1=s1, scalar2=s2, op0=op0, op1=op1), reads, writes)

    def stt(eng, out, in0, scalar, in1, op0, op1, reads, writes):
        return P.op(eng, lambda e: e.scalar_tensor_tensor(out=out, in0=in0, scalar=scalar, in1=in1, op0=op0, op1=op1), reads, writes)

    def cp(eng, out, in_, reads, writes):
        if eng == "act":
            return P.op("act", lambda e: e.copy(out=out, in_=in_), reads, writes)
        return P.op(eng, lambda e: e.tensor_copy(out=out, in_=in_), reads, writes)

    def rcp(out, in_, reads, writes):
        return P.op("dve", lambda e: e.reciprocal(out=out, in_=in_), reads, writes)

    def mset(eng, out, val, reads, writes):
        return P.op(eng, lambda e: e.memset(out, val), reads, writes)

    def red(out, in_, reads, writes, absv=False):
        if absv:
            return P.op("dve", lambda e: e.tensor_reduce(out=out, in_=in_, axis=AX.X, op=ALU.add, apply_absolute_value=True), reads, writes)
        return P.op("dve", lambda e: e.tensor_reduce(out=out, in_=in_, axis=AX.X, op=ALU.add), reads, writes)

    def rsqrt_chain(dst, src, scale, n_key, reads, tmp):
        ts("dve", tmp, src, scale, EPS, ALU.mult, ALU.add, reads, [n_key + ("t",)])
        act(tmp, tmp, AF.Sqrt, [n_key + ("t",)], [n_key + ("t",)])
        rcp(dst, tmp, [n_key + ("t",)], [n_key])

    P.dma("pool", ident, eye_d.ap(), writes=["ident"])
    P.dma("pool", jmat, jmat_d.ap(), writes=["jmat"])
    mset("dve", ones_f, 1.0, [], ["ones_f"])
    mset("dve", ones_b, 1.0, [], ["ones_b"])
    P.dma("sp", cT_sb, cT_d.ap(), writes=["cT"])
    P.dma("sp", ndel, ndel_d.ap(), writes=["ndel"])
    act(sc_col, cT_sb, AF.Silu, ["cT"], ["sc"])

    for l in range(NL):
        xsrc = x_in if l == 0 else xres_d
        xdst = out_d if l == NL - 1 else xres_d
        lambda_init = 0.8 - 0.6 * math.exp(-0.3 * l)
        wo_pref = None

        if "M" in stages:
            P.barrier()
            ar.off = ppos
            wada = [ar.alloc([3072], F32) for _ in range(2)]
            row = ar.alloc([3072], F32)
            brow = ar.alloc([3072], F32)
            grow = ar.alloc([1024], F32)
            arow = ar.alloc([1024], F32)
            gnrow = ar.alloc([1792], F32)
            lamrow = ar.alloc([128], F32)
            lamt = ar.alloc([8], F32)
            mset("dve", lamt, 0.0, [], ["lamrow_init"])

            P.dma("sp", brow[0:1, :], bada_d[l], writes=["brow"])
            P.dma("sp", grow[0:1, :], normg_d[l], writes=["grow"])
            P.dma("sp", gnrow[0:1, :], gnrow_d[l], writes=["gnrow"])
            P.dma("sp", lamrow[0:1, :], lam_d[l], writes=["lamrow"])
            P.dma("sp", hycw, hycw_d[l].rearrange("p a b -> p (a b)"), writes=["hycw"])
            P.dma("sp", hycb, hycb_d[l], writes=["hycb"])
            P.dma("sp", sccw, sccw_d[l].rearrange("p a b -> p (a b)"), writes=["sccw"])
            P.dma("sp", hybias, hybias_d[l], writes=["hybias"])
            P.dma("sp", subg[0:64, :], subln_d[l], writes=["subg"])
            ts("dve", subg[0:64, :], subg[0:64, :], 1.0 - lambda_init, None, ALU.mult, ALU.bypass, ["subg"], ["subg"])

            for kc in range(8):
                P.dma("sp", wada[kc % 2], wada_d[l, kc * 128:(kc + 1) * 128, :], writes=[("wada", kc % 2)])
                for n in range(6):
                    mm(bk(n)[0:1, :], sc_col[:, kc:kc + 1], wada[kc % 2][:, n * 512:(n + 1) * 512],
                       kc == 0, kc == 7, ["sc", ("wada", kc % 2)], [("bank", n)])
            for n in range(6):
                tt("dve", row[0:1, n * 512:(n + 1) * 512], bk(n)[0:1, :], brow[0:1, n * 512:(n + 1) * 512], ALU.add,
                   [("bank", n), "brow"], [("row", n)])
            stt("dve", arow[0:1, :], row[0:1, 1024:2048], 1.0, grow[0:1, :], ALU.add, ALU.mult,
                [("row", 2), ("row", 3), "grow"], ["arow"])
            bsrc = [(arow, 0, A_bc, "A_bc", ["arow"]), (row, 0, SH_bc, "SH_bc", [("row", 0), ("row", 1)]),
                    (row, 2048, G_bc, "G_bc", [("row", 4), ("row", 5)])]
            bi = 6
            for (src, so, dst, key, rd) in bsrc:
                for h in range(2):
                    b = 6 + (bi % 2)
                    bi += 1
                    mm(bk(b), ones_f[0:1, :], src[0:1, so + h * 512: so + (h + 1) * 512], True, True,
                       ["ones_f"] + rd, [("bank", b)])
                    cp("act", dst[:, h * 512:(h + 1) * 512], bk(b), [("bank", b)], [(key, h)])
            for (lo, hi) in ((0, 512), (512, 1024), (1024, 1536), (1536, 1792)):
                b = 6 + (bi % 2)
                bi += 1
                mm(bk(b)[:, 0:hi - lo], ones_f[0:1, :], gnrow[0:1, lo:hi], True, True, ["ones_f", "gnrow"], [("bank", b)])
                cp("act", GN_bc[:, lo:hi], bk(b)[:, 0:hi - lo], [("bank", b)], ["GN_bc"])
            lr = lamrow[0:1, :].rearrange("p (a b) -> p a b", b=32)
            prod = ar.alloc([64], F32)
            tt("dve", prod[0:1, 0:32], lr[:, 0, :], lr[:, 1, :], ALU.mult, ["lamrow"], ["prod0"])
            tt("dve", prod[0:1, 32:64], lr[:, 2, :], lr[:, 3, :], ALU.mult, ["lamrow"], ["prod1"])
            red(lamt[0:1, 2:4], prod[0:1, :].rearrange("p (a b) -> p a b", b=32), ["prod0", "prod1"], ["lamsum"])
            act(lamt[0:1, 4:6], lamt[0:1, 2:4], AF.Exp, ["lamsum"], ["lamexp"])
            stt("dve", lamt[0:1, 6:7], lamt[0:1, 5:6], -lambda_init, lamt[0:1, 4:5], ALU.add, ALU.subtract,
                ["lamexp"], ["neglam_row"])
            mm(bk(6)[0:64, 0:2], ones_f[0:1, 0:64], lamt[0:1, 6:8], True, True, ["ones_f", "neglam_row"], [("bank", 6)])
            cp("act", neglam[0:64, :], bk(6)[0:64, 0:1], [("bank", 6)], ["neglam"])

        if "F" in stages:
            P.barrier()
            ar.off = ppos
            w1 = ar.alloc([64], F32)
            w2 = ar.alloc([64], F32)
            w3 = ar.alloc([512], F32)
            b1c = ar.alloc([1], F32)
            frc = ar.alloc([1], F32)
            b2c = ar.alloc([1], F32)
            tposr = ar.alloc([L2], F32)
            embc = [ar.alloc([512], F32) for _ in range(2)]
            argb = [ar.alloc([512], F32) for _ in range(2)]
            tmpb = [ar.alloc([512], F32) for _ in range(2)]
            h1s = [ar.alloc([512], F32) for _ in range(2)]
            h2s = ar.alloc([L2], F32)
            winb = [ar.alloc([512], F32) for _ in range(2)]
            kbuf2 = [ar.alloc([L2], F32) for _ in range(2)]
            gb = ar.alloc([L2], BF16)
            nrm = ar.alloc([4], F32)

            P.dma("sp", w1[0:33, :], hyw1_d[l], writes=["w1"])
            P.dma("sp", w2[0:64, :], hyw2_d[l], writes=["w2"])
            P.dma("sp", w3[0:64, :], hyw3_d[l], writes=["w3"])
            P.dma("sp", b1c[0:64, :], hyb1_d[l], writes=["b1c"])
            P.dma("sp", frc[0:64, :], hyfr_d[l], writes=["frc"])
            P.dma("sp", b2c[0:64, :], hyb2_d[l], writes=["b2c"])
            P.dma("sp", tposr[0:1, :], tpos_d.ap(), writes=["tposr"])

            def sin_layer(ps, bcol, dst, ci, tag, wkey):
                a = argb[ci % 2][0:64, :]
                t = tmpb[ci % 2][0:64, :]
                ka, kt = ("argb", ci % 2), ("tmpb", ci % 2)
                ts("dve", a, ps, bcol[0:64, :], frc[0:64, :], ALU.add, ALU.mult, [tag, "b1c", "b2c", "frc"], [ka])
                ts("dve", t, a, 1.0 / TWO_PI, MAGIC, ALU.mult, ALU.add, [ka], [kt])
                ts("dve", t, t, -MAGIC, -TWO_PI, ALU.add, ALU.mult, [kt], [kt])
                tt("dve", a, a, t, ALU.add, [ka, kt], [ka])
                ts("dve", a, a, 3.1415925, -3.1415925, ALU.min, ALU.max, [ka], [ka])
                act(dst, a, AF.Sin, [ka], [wkey])

            for c0_ in range(0, NF, 2):
                wave = [ci for ci in (c0_, c0_ + 1) if ci < NF]
                for ci in wave:
                    e = embc[ci % 2]
                    P.dma("sp", e[0:33, :], emb_d[:, ci * 512:(ci + 1) * 512], writes=[("embc", ci % 2)])
                    b = ci % 2
                    mm(bk(b)[0:64, :], w1[0:33, :], e[0:33, :], True, True, ["w1", ("embc", ci % 2)], [("bank", b)])
                for ci in wave:
                    b = ci % 2
                    sin_layer(bk(b)[0:64, :], b1c, h1s[ci % 2][0:64, :], ci, ("bank", b), ("h1s", ci % 2))
                for ci in wave:
                    b = ci % 2
                    mm(bk(2 + b)[0:64, :], w2[0:64, :], h1s[ci % 2][0:64, :], True, True, ["w2", ("h1s", ci % 2)], [("bank", 2 + b)])
                for ci in wave:
                    b = ci % 2
                    sin_layer(bk(2 + b)[0:64, :], b2c, h2s[0:64, ci * 512:(ci + 1) * 512], ci, ("bank", 2 + b), ("h2s", ci))

            for ci in range(NF):
                fwd = (ci * 512) < S
                b2 = 6 + ci % 2
                mm(bk(b2), ones_f[0:1, :], tposr[0:1, ci * 512:(ci + 1) * 512], True, True,
                   ["ones_f", "tposr"], [("bank", b2)])
                for hh in range(2):
                    wc = (0 if fwd else 256) + hh * 128
                    b = 4 + hh
                    mm(bk(b), w3[0:64, wc:wc + 128], h2s[0:64, ci * 512:(ci + 1) * 512], True, True,
                       ["w3", ("h2s", ci)], [("bank", b)])
                    wv = winb[hh]
                    act(wv, bk(b2), AF.Exp, [("bank", b2), "ndel"], [("winb", hh)], scale=ndel[:, hh:hh + 1])
                    stt("dve", kbuf2[hh][:, ci * 512:(ci + 1) * 512], wv, 0.05, bk(b), ALU.add, ALU.mult,
                        [("winb", hh), ("bank", b)], [("kbuf", hh, ci)])
            for hh in range(2):
                kb_ = kbuf2[hh]
                mset("dve", kb_[:, L2 - 1:L2], 0.0, [("kbuf", hh, NF - 1)], [("kbuf", hh, NF - 1)])
                kall = [("kbuf", hh, ci) for ci in range(NF)]
                red(nrm[:, 2 * hh:2 * hh + 1], kb_, kall, [("nrm0", hh)], absv=True)
                rcp(nrm[:, 2 * hh + 1:2 * hh + 2], nrm[:, 2 * hh:2 * hh + 1], [("nrm0", hh)], [("nrm1", hh)])
                ts("dve", gb, kb_, nrm[:, 2 * hh + 1:2 * hh + 2], None, ALU.mult, ALU.bypass, kall + [("nrm1", hh)], ["gb"])
                P.dma("sp", gflt_d[hh * 128:(hh + 1) * 128, :], gb, reads=["gb"], writes=[uq("gflt")])

        if "1" in stages:
            P.barrier()
            ar.off = ppos
            winb16 = ar.alloc([8 * D_IN], BF16)
            win3 = winb16.rearrange("p (k c) -> p k c", k=8)
            xt = [ar.alloc([1024], F32) for _ in range(4)]
            junk = ar.alloc([1024], BF16)
            hb = [ar.alloc([1024], BF16) for _ in range(4)]
            hT = [ar.alloc([8 * 512], BF16) for _ in range(3)]
            hT3 = [h.rearrange("p (k t) -> p k t", k=8) for h in hT]
            st8 = ar.alloc([16], F32)
            ropet = [ar.alloc([192], F32) for _ in range(2)]
            GW = {"A": 384, "C": 512}
            bufS = {g_: [ar.alloc([GW[g_]], F32) for _ in range(2)] for g_ in "AC"}
            bufQ = {g_: [ar.alloc([GW[g_]], F32) for _ in range(2)] for g_ in "AC"}
            stat = {g_: [ar.alloc([48], F32) for _ in range(2)] for g_ in "AC"}
            tabC = {g_: ar.alloc([GW[g_]], F32) for g_ in "AC"}
            tabS = {g_: ar.alloc([GW[g_]], F32) for g_ in "AC"}
            qkb = {g_: [ar.alloc([GW[g_]], BF16) for _ in range(2)] for g_ in "AC"}
            qkAT = [ar.alloc([3 * 512], BF16) for _ in range(2)]
            qkCT = [ar.alloc([4 * 512], BF16) for _ in range(2)]
            vAs = [ar.alloc([4 * 130], BF16) for _ in range(2)]
            vCs = [ar.alloc([4 * 260], BF16) for _ in range(2)]
            pre = [ar.alloc([514], F32) for _ in range(4)]
            cva = ar.alloc([512], F32)
            cvb = ar.alloc([512], F32)
            cvc = ar.alloc([512], F32)
            sgt = ar.alloc([512], F32)
            fst = [ar.alloc([512], BF16) for _ in range(4)]
            fst_i = [0]

            for kc in range(8):
                for hcol in range(2):
                    P.dma("pool", win3[:, kc, hcol * 1920:(hcol + 1) * 1920],
                          win_d[l, kc * 128:(kc + 1) * 128, hcol * 1920:(hcol + 1) * 1920], writes=[("win", kc, hcol)])
            win_keys = [("win", kc, hcol) for kc in range(8) for hcol in range(2)]
            for s_ in range(2):
                va = vAs[s_].rearrange("p (t h c) -> p t h c", t=4, h=2)
                vc = vCs[s_].rearrange("p (t h c) -> p t h c", t=4, h=4)
                mset("pool", va[:, :, :, 64:65], 1.0, [], [("vAs1", s_)])
                mset("pool", vc[:, :, :, 64:65], 1.0, [], [("vCs1", s_)])

            def Hcompute(n):
                for ti in range(4):
                    T = 4 * n + ti
                    xs = T % 4
                    P.dma("sp", xt[xs], xsrc[T * 128:(T + 1) * 128, :], reads=[("xout", l - 1, T)], writes=[("xt", xs)])
                    act(junk, xt[xs], AF.Square, [("xt", xs)], ["junk", ("ss", xs)], accum_out=st8[:, xs:xs + 1])
                    ts("dve", st8[:, 4 + xs:5 + xs], st8[:, xs:xs + 1], 1.0 / D_MODEL, EPS, ALU.mult, ALU.add, [("ss", xs)], [("ms", xs)])
                    act(st8[:, 8 + xs:9 + xs], st8[:, 4 + xs:5 + xs], AF.Ln, [("ms", xs)], [("lnms", xs)])
                    act(st8[:, 12 + xs:13 + xs], st8[:, 8 + xs:9 + xs], AF.Exp, [("lnms", xs)], [("rstd", xs)], scale=-0.5)
                    stt("dve", xt[xs], xt[xs], st8[:, 12 + xs:13 + xs], A_bc, ALU.mult, ALU.mult,
                        [("xt", xs), ("rstd", xs), ("A_bc", 0), ("A_bc", 1)], [("xt", xs)])
                    tt("pool", hb[xs], xt[xs], SH_bc, ALU.add, [("xt", xs), ("SH_bc", 0), ("SH_bc", 1)], [("hb", xs)])

            def Htranspose(n):
                sl = n % 3
                for ti in range(4):
                    T = 4 * n + ti
                    xs = T % 4
                    pb_ = 0 if ti % 2 == 0 else 5
                    pt = bkb(pb_).rearrange("p (k t) -> p k t", k=8)
                    for kc in range(8):
                        tr(pt[:, kc, :], hb[xs][:, kc * 128:(kc + 1) * 128], [("hb", xs)], [("bank", pb_)])
                    cp("act", hT3[sl][:, :, ti * 128:(ti + 1) * 128], pt, [("bank", pb_)], [("hT", sl, ti)])

            def tokmm(n, ti):
                sl = n % 3
                lh = lambda kc: hT3[sl][:, kc, ti * 128:(ti + 1) * 128]
                for kc in range(8):
                    mm(bk(1), lh(kc), win3[:, kc, 0:512], kc == 0, kc == 7, [("hT", sl, ti)] + win_keys, [("bank", 1)])
                for kc in range(8):
                    mm(bk(2), lh(kc), win3[:, kc, O_CQ:O_CQ + 512], kc == 0, kc == 7, [("hT", sl, ti)] + win_keys, [("bank", 2)])
                for kc in range(8):
                    mm(bk(3)[:, 0:256], lh(kc), win3[:, kc, O_CV:O_CV + 256], kc == 0, kc == 7, [("hT", sl, ti)] + win_keys, [("bank", 3)])

            def post_grp(g_, T, ps_bank, nh, hd, gn_c, gn_s, rope_cos, rope_sin):
                p_ = T % 2
                width = GW[g_]
                rt = ropet[p_]
                kb = ("bank", ps_bank)
                bS, bQ, sv = bufS[g_][p_], bufQ[g_][p_], stat[g_][p_]
                kS, kQ, kst = ("bufS", g_, p_), ("bufQ", g_, p_), ("stat", g_, p_)
                psv = bk(ps_bank)[:, 0:width]
                v3 = lambda a: a.rearrange("p (h d) -> p h d", d=hd)
                cos_b = cust(rt, rope_cos, [[0, nh], [1, hd]])
                sin_b = cust(rt, rope_sin, [[0, nh], [1, hd]])
                tt("pool", v3(tabC[g_]), cos_b, v3(GN_bc[:, gn_c:gn_c + width]), ALU.mult, [("ropet", p_), "GN_bc"], [("tabC", g_)])
                tt("pool", v3(tabS[g_]), sin_b, v3(GN_bc[:, gn_s:gn_s + width]), ALU.mult, [("ropet", p_), "GN_bc"], [("tabS", g_)])
                yield
                act(bS, psv, AF.Square, [kb], [kS, (kS, 0), (kS, 1)])
                yield
                red(sv[:, 0:nh], v3(bS), [kS], [kst])
                ts("dve", sv[:, 0:nh], sv[:, 0:nh], 1.0 / hd, EPS, ALU.mult, ALU.add, [kst], [kst])
                yield
                act(sv[:, 16:16 + nh], sv[:, 0:nh], AF.Ln, [kst], [kst])
                act(sv[:, 32:32 + nh], sv[:, 16:16 + nh], AF.Exp, [kst], [kst], scale=-0.5)
                yield
                rs_b = cust(sv, 32, [[1, nh], [0, hd]])
                tt("dve", v3(bQ), v3(psv), rs_b, ALU.mult, [kb, kst], [kQ])
                yield
                hw = 16
                nb = hd // 32
                q5 = bQ.rearrange("p (g two w) -> p g two w", two=2, w=hw)
                o5 = bS.rearrange("p (g two w) -> p g two w", two=2, w=hw)
                t5 = tabS[g_].rearrange("p (g two w) -> p g two w", two=2, w=hw)
                for half in range(2):
                    tt("pool" if half else "dve", o5[:, :, half, :], q5[:, :, 1 - half, :], t5[:, :, half, :], ALU.mult,
                       [kQ, ("tabS", g_)], [(kS, half)])
                yield
                tt("dve", bQ, bQ, tabC[g_], ALU.mult, [kQ, ("tabC", g_)], [kQ])
                tt("dve", qkb[g_][p_], bQ, bS, ALU.add, [kQ, (kS, 0), (kS, 1)], [("qkb", g_, p_)])

            def post_tile(n, ti):
                T = 4 * n + ti
                cs = n % 2
                va = vAs[cs].rearrange("p (t h c) -> p t h c", t=4, h=2)
                vc = vCs[cs].rearrange("p (t h c) -> p t h c", t=4, h=4)
                P.dma("sp", ropet[T % 2], rope_d[T * 128:(T + 1) * 128, :], writes=[("ropet", T % 2)])
                gens = [post_grp("A", T, 1, 6, 64, 0, 896, 0, 64), post_grp("C", T, 2, 16, 32, 384, 1280, 128, 160)]
                step = 0
                while gens:
                    for g__ in list(gens):
                        try:
                            next(g__)
                        except StopIteration:
                            gens.remove(g__)
                    step += 1
                    if step == 2:
                        cp("act", va[:, ti, :, 0:64], bk(1)[:, 384:512].rearrange("p (h d) -> p h d", h=2), [("bank", 1)], [("vAs", cs, ti)])
                        cp("act", vc[:, ti, :, 0:64], bk(3)[:, 0:256].rearrange("p (h d) -> p h d", h=4), [("bank", 3)], [("vCs", cs, ti)])

            def qk_transposes(n, ti):
                T = 4 * n + ti
                cs = n % 2
                p_ = T % 2
                qat = qkAT[cs].rearrange("p (b t) -> p b t", b=3)
                qct = qkCT[cs].rearrange("p (b t) -> p b t", b=4)
                ptb = bkb(4).rearrange("p (b t) -> p b t", b=8)
                for b3 in range(3):
                    tr(ptb[:, b3, :], qkb["A"][p_][:, b3 * 128:(b3 + 1) * 128], [("qkb", "A", p_)], [("bank", 4)])
                cp("act", qat[:, :, ti * 128:(ti + 1) * 128], ptb[:, 0:3, :], [("bank", 4)], [("qkAT", cs, ti)])
                ptc = bkb(5).rearrange("p (b t) -> p b t", b=8)
                for b4 in range(4):
                    tr(ptc[:, b4, :], qkb["C"][p_][:, b4 * 128:(b4 + 1) * 128], [("qkb", "C", p_)], [("bank", 5)])
                cp("act", qct[:, :, ti * 128:(ti + 1) * 128], ptc[:, 0:4, :], [("bank", 5)], [("qkCT", cs, ti)])

            def tok_stores(n):
                cs = n % 2
                qat = qkAT[cs].rearrange("p (b t) -> p b t", b=3)
                qct = qkCT[cs].rearrange("p (b t) -> p b t", b=4)
                tk = lambda nm: [(nm, cs, ti) for ti in range(4)]
                c0, c1 = n * 512, (n + 1) * 512
                P.dma("sp", qTA_d[0:128, c0:c1], qat[:, 0, :], reads=tk("qkAT"), writes=[uq("qTA_d")])
                P.dma("sp", qTA_d[128:256, c0:c1], qat[:, 1, :], reads=tk("qkAT"), writes=[uq("qTA_d")])
                for kv in range(2):
                    for u in range(2):
                        P.dma("sp", kTA_d[kv * 128 + u * 64:kv * 128 + (u + 1) * 64, c0:c1], qat[kv * 64:(kv + 1) * 64, 2, :],
                              reads=tk("qkAT"), writes=[uq("kTA_d")])
                P.dma("sp", qTC_d[0:128, c0:c1], qct[:, 0, :], reads=tk("qkCT"), writes=[uq("qTC_d")])
                P.dma("sp", qTC_d[128:256, c0:c1], qct[:, 1, :], reads=tk("qkCT"), writes=[uq("qTC_d")])
                P.dma("sp", kTC_d[0:128, c0:c1], qct[:, 2, :], reads=tk("qkCT"), writes=[uq("kTC_d")])
                P.dma("sp", kTC_d[128:256, c0:c1], qct[:, 3, :], reads=tk("qkCT"), writes=[uq("kTC_d")])
                P.dma("sp", vA_d[c0:c1, :].rearrange("(t p) c -> p t c", p=128), vAs[cs].rearrange("p (t c) -> p t c", t=4),
                      reads=tk("vAs") + [("vAs1", cs)], writes=[uq("vA_d")])
                P.dma("sp", vC_d[c0:c1, :].rearrange("(t p) c -> p t c", p=128), vCs[cs].rearrange("p (t c) -> p t c", t=4),
                      reads=tk("vCs") + [("vCs1", cs)], writes=[uq("vC_d")])

            fb = [0]

            def feat_mm(n, col):
                sl = n % 3
                b = 6 + (fb[0] % 2)
                fb[0] += 1
                hk = [("hT", sl, ti) for ti in range(4)]
                for kc in range(8):
                    mm(bk(b), win3[:, kc, col:col + 128], hT3[sl][:, kc, :], kc == 0, kc == 7, hk + win_keys, [("bank", b)])
                return b

            def halo_mm(n, col, slot):
                res = []
                for side in range(2):
                    nn = n - 1 if side == 0 else n + 1
                    if nn < 0 or nn >= NCH:
                        res.append(None)
                        continue
                    sl = nn % 3
                    tcol = 511 if side == 0 else 0
                    ti = 3 if side == 0 else 0
                    o = bk(5)[:, 256 + slot * 2 + side: 256 + slot * 2 + side + 1]
                    for kc in range(8):
                        mm(o, win3[:, kc, col:col + 128], hT3[sl][:, kc, tcol:tcol + 1], kc == 0, kc == 7,
                           [("hT", sl, ti)] + win_keys, [("bank", 5)], skip_group_check=True)
                    res.append(o)
                return res

            def stage_out(dram_ap, src_fn, q="pool"):
                i = fst_i[0] % 4
                fst_i[0] += 1
                src_fn(fst[i], ("fst", i))
                P.dma(q, dram_ap, fst[i], reads=[("fst", i)], writes=[uq("scr_out")])

            def load_pre(n, col, pslot, hslot):
                b = feat_mm(n, col)
                hal = halo_mm(n, col, hslot)
                pk = ("pre", pslot)
                cp("act", pre[pslot][:, 1:513], bk(b), [("bank", b)], [pk])
                for side in range(2):
                    c = 0 if side == 0 else 513
                    if hal[side] is None:
                        mset("dve", pre[pslot][:, c:c + 1], 0.0, [pk], [pk])
                    else:
                        cp("act", pre[pslot][:, c:c + 1], hal[side], [("bank", 5), pk], [pk])
                return pk

            def conv3(dst, pslot, wt, wi, bias, reads, wkey):
                p_ = pre[pslot]
                if bias is not None:
                    ts("dve", dst, p_[:, 0:512], wt[:, wi * 3:wi * 3 + 1], bias, ALU.mult, ALU.add, reads, [wkey])
                else:
                    ts("dve", dst, p_[:, 0:512], wt[:, wi * 3:wi * 3 + 1], None, ALU.mult, ALU.bypass, reads, [wkey])
                stt("dve", dst, p_[:, 1:513], wt[:, wi * 3 + 1:wi * 3 + 2], dst, ALU.mult, ALU.add, reads + [wkey], [wkey])
                stt("dve", dst, p_[:, 2:514], wt[:, wi * 3 + 2:wi * 3 + 3], dst, ALU.mult, ALU.add, reads + [wkey], [wkey])

            def feat_gates(n):
                c0, c1 = n * 512, (n + 1) * 512
                for (col, dd) in ((O_AG, sgA_d), (O_CG, sgC_d)):
                    for hh in range(2):
                        b = feat_mm(n, col + hh * 128)
                        stage_out(dd[hh * 128:(hh + 1) * 128, c0:c1],
                                  lambda dst, key, b=b: act(dst, bk(b), AF.Silu, [("bank", b)], [key]), q="sp")

            def feat_B(n, hh):
                c0, c1 = n * 512, (n + 1) * 512
                k1 = load_pre(n, O_BP + (2 + hh) * 128, 0, 0)
                conv3(cva, 0, hycw, 2 + hh, hycb[:, 2 + hh:3 + hh], [k1, "hycw", "hycb"], "cva")
                k2 = load_pre(n, O_BP + (4 + hh) * 128, 1, 1)
                conv3(cvb, 1, hycw, 4 + hh, hycb[:, 4 + hh:5 + hh], [k2, "hycw", "hycb"], "cvb")
                stage_out(zT_d[hh * 128:(hh + 1) * 128, c0:c1],
                          lambda dst, key: tt("pool", dst, cva, cvb, ALU.mult, ["cva", "cvb"], [key]))
                k0 = load_pre(n, O_BP + hh * 128, 2, 2)
                conv3(cvc, 2, hycw, hh, hycb[:, hh:hh + 1], [k0, "hycw", "hycb"], "cvc")
                b = feat_mm(n, O_BG + hh * 128)
                act(sgt, bk(b), AF.Silu, [("bank", b)], ["sgt"])
                stage_out(xgB_d[hh * 128:(hh + 1) * 128, c0:c1],
                          lambda dst, key: tt("pool", dst, cvc, sgt, ALU.mult, ["cvc", "sgt"], [key]))

            def feat_D(n):
                c0, c1 = n * 512, (n + 1) * 512
                for hh in range(2):
                    k1 = load_pre(n, O_DP + (2 + hh) * 128, 0, 3)
                    k2 = load_pre(n, O_DP + (4 + hh) * 128, 1, 4)
                    tt("pool", pre[0], pre[0], pre[1], ALU.mult, [k1, k2], [k1])
                    conv3(cva, 0, sccw, hh, None, [k1, "sccw"], "cva")
                    b = feat_mm(n, O_DP + hh * 128)
                    tt("dve", cvb, cva, bk(b), ALU.mult, ["cva", ("bank", b)], ["cvb"])
                    b = feat_mm(n, O_DG + hh * 128)
                    act(sgt, bk(b), AF.Silu, [("bank", b)], ["sgt"])
                    stage_out(yT_d[768 + hh * 128:768 + (hh + 1) * 128, c0:c1],
                              lambda dst, key: tt("pool", dst, cvb, sgt, ALU.mult, ["cvb", "sgt"], [key]))

            Hcompute(0)
            Htranspose(0)
            if NCH > 1:
                Hcompute(1)
            dq = []

            stq = []

            def run_dq():
                for e_ in stq:
                    e_[0] -= 1
                while stq and stq[0][0] <= 0:
                    tok_stores(stq.pop(0)[1])
                if dq:
                    n_, ti_ = dq.pop(0)
                    qk_transposes(n_, ti_)
                    if ti_ == 3:
                        stq.append([2, n_])

            for n in range(NCH):
                for ti in range(4):
                    tokmm(n, ti)
                    post_tile(n, ti)
                    if ti == 0:
                        if n + 1 < NCH:
                            Htranspose(n + 1)
                        if n + 2 < NCH:
                            Hcompute(n + 2)
                        feat_gates(n)
                    elif ti == 1:
                        feat_B(n, 0)
                    elif ti == 2:
                        feat_B(n, 1)
                    else:
                        feat_D(n)
                    run_dq()
                    dq.append((n, ti))
            while dq:
                run_dq()
            while stq:
                tok_stores(stq.pop(0)[1])

        if "A" in stages:
            P.barrier()
            ar.off = ppos
            ztok = ar.alloc([256 * NT], BF16)
            ztok3 = ztok.rearrange("p (c s) -> p c s", c=256)
            ypr = ar.alloc([NT * 256], BF16)
            ypr3 = ypr.rearrange("p (t c) -> p t c", t=NT)
            NTB = 3
            tb_off0 = ar.off
            tbuf = [ar.alloc([W], BF16) for _ in range(NTB)]
            xgc = [ar.alloc([512], BF16) for _ in range(2)]
            zcc = [ar.alloc([512], BF16) for _ in range(2)]
            et = ar.alloc([512], F32)
            ystB = [ar.alloc([512], BF16) for _ in range(2)]
            B_base = ar.off
            zc = [ar.alloc([2 * 512], BF16) for _ in range(2)]
            for n in range(NCH):
                zt = zc[n % 2].rearrange("p (h t) -> p h t", h=2)
                for hh in range(2):
                    P.dma("sp", zt[:, hh, :], zT_d[hh * 128:(hh + 1) * 128, n * 512:(n + 1) * 512], reads=["scr_out"], writes=[("zc", n % 2, hh)])
                b = n % 2
                pz = bkb(b).rearrange("p (t h c) -> p t h c", t=4, h=2)
                for ti in range(4):
                    for hh in range(2):
                        tr(pz[:, ti, hh, :], zt[:, hh, ti * 128:(ti + 1) * 128], [("zc", n % 2, hh)], [("bank", b)])
                dst = AP(ztok.tensor, ztok.offset + 4 * n, [list(ztok.ap[0]), [1, 4], [128 * NT, 2], [NT, 128]])
                cp("dve", dst, pz, [("bank", b)], [("ztok", n)])
            zkeys = [("ztok", n) for n in range(NCH)]
            Ds = [0] + [d for k in range(1, NT) for d in (k, -k)]
            NG = 256 // CPB

            NSB = 12 * NCH * NT
            BSKIP = 3
            BCH = max(1, -(-(256 * (len(Ds) + BSKIP * 0)) * BSKIP // max(1, NSB - 64)) + 1)

            def B_gen():
                for g in range(NG):
                    b = 7
                    py = bk(b)[:, 0:CPB * NT].rearrange("p (c t) -> p c t", c=CPB)
                    for cl in range(CPB):
                        c = g * CPB + cl
                        slot = c % NTB
                        P.dma("sp", tbuf[slot], AP(gflt_d, c * L2, [[1, 128], [1, W]]), reads=["gflt"], writes=[("tbuf", slot)])
                        for di, D in enumerate(Ds):
                            tlo, thi = max(0, D), min(NT - 1, NT - 1 + D)
                            n0 = (S - 128) - 128 * D
                            mm(py[:, cl, tlo:thi + 1], tbuf[slot][:, n0:n0 + 128], ztok3[:, c, tlo - D:thi - D + 1],
                               di == 0, di == len(Ds) - 1, [("tbuf", slot)] + zkeys, [("bank", b)], skip_group_check=True)
                            if di % BCH == BCH - 1:
                                yield
                        if cl == CPB - 1:
                            dst = AP(ypr.tensor, ypr.offset + g * CPB, [list(ypr.ap[0]), [1, CPB], [256, NT]])
                            cp("dve", dst, py, [("bank", b)], [("ypr", g)])
                        yield

            bgen = B_gen()
            sbc = [0]
            bcall = [0]

            def bstep():
                bcall[0] += 1
                if bcall[0] % BSKIP:
                    return
                try:
                    next(bgen)
                except StopIteration:
                    pass

            ar.off = B_base
            AUX = 6
            kdup = ar.alloc([S], BF16)
            qpair = ar.alloc([S], BF16)
            vsb = ar.alloc([NT * 130], BF16)
            vsb4 = vsb.rearrange("p (t h c) -> p t h c", t=NT, h=2)
            pTd = [ar.alloc([1024], BF16) for _ in range(2)]
            osbA = [ar.alloc([512], F32) for _ in range(2)]
            rden = ar.alloc([512], F32)
            bden = ar.alloc([512], F32)
            otmp = ar.alloc([512], F32)
            sgl = [[ar.alloc([512], BF16) for _ in range(2)] for _ in range(2)]
            yst = [ar.alloc([512], BF16) for _ in range(2)]
            scale_a = 64 ** -0.5
            it = 0
            pend = []

            def flush():
                while pend:
                    pend.pop(0)()

            for q4 in range(NT // 4):
                P.dma("sp", vsb.rearrange("p (t c) -> p t c", t=NT)[:, q4 * 4:(q4 + 1) * 4, :],
                      vA_d[q4 * 512:(q4 + 1) * 512, :].rearrange("(t p) c -> p t c", p=128), reads=["vA_d"], writes=["vsb"])
            for g in range(2):
                P.dma("sp", kdup, kTA_d[g * 128:(g + 1) * 128, :], reads=["kTA_d"], writes=["kdup"])
                P.dma("sp", qpair, qTA_d[g * 128:(g + 1) * 128, :], reads=["qTA_d"], writes=["qpair"])
                for qi in range(NCH):
                    s2 = it % 2
                    for hh in range(2):
                        h_ = 2 * g + hh
                        P.dma("sp", sgl[s2][hh][0:64, :], sgA_d[h_ * 64:(h_ + 1) * 64, qi * 512:(qi + 1) * 512], reads=["scr_out"], writes=[("sgl", s2, hh)])

                    def qk(sb, qi=qi):
                        p_ = sb % 2
                        for hh in range(2):
                            b = 2 * p_ + hh
                            rows = slice(hh * 64, (hh + 1) * 64)
                            mm(bk(b), kdup[rows, sb * 128:(sb + 1) * 128], qpair[rows, qi * 512:(qi + 1) * 512], True, True,
                               ["kdup", "qpair"], [("bank", b)])

                    qk(0)
                    if NT > 1:
                        qk(1)
                    for sb in range(NT):
                        p_ = sb % 2
                        for hh in range(2):
                            act(pTd[p_][:, hh * 512:(hh + 1) * 512], bk(2 * p_ + hh), AF.Exp, [("bank", 2 * p_ + hh)], [("pT", 2 * p_ + hh)], scale=scale_a)
                        if pend and sb % 4 == 1:
                            pend.pop(0)()
                        for hh in range(2):
                            if hh == 1 and sb + 2 < NT:
                                qk(sb + 2)
                            mm(bk(4 + hh)[0:65, :], vsb4[:, sb, g, :], pTd[p_][:, hh * 512:(hh + 1) * 512], sb == 0, sb == NT - 1,
                               ["vsb", ("pT", 2 * p_ + hh)], [("bank", 4 + hh)])
                            bstep()
                    flush()
                    for hh in range(2):
                        cp("dve", osbA[hh][0:65, :], bk(4 + hh)[0:65, :], [("bank", 4 + hh)], [("osbA", hh)])
                    for hh in range(2):
                        h = 2 * g + hh

                        def epi1(hh=hh):
                            rcp(rden[64:65, :], osbA[hh][64:65, :], [("osbA", hh)], ["rden"])
                            mm(bk(AUX)[0:64, :], ones_f[64:65, 0:64], rden[64:65, :], True, True, ["ones_f", "rden"], [("bank", AUX)])

                        def epi2(hh=hh, h=h, qi=qi, s2=s2):
                            cp("dve", bden[0:64, :], bk(AUX)[0:64, :], [("bank", AUX)], ["bden"])
                            tt("dve", otmp[0:64, :], osbA[hh][0:64, :], bden[0:64, :], ALU.mult, [("osbA", hh), "bden"], ["otmp"])
                            tt("pool", yst[hh][0:64, :], otmp[0:64, :], sgl[s2][hh][0:64, :], ALU.mult, ["otmp", ("sgl", s2, hh)], [("yst", hh)])
                            P.dma("pool", yT_d[h * 64:(h + 1) * 64, qi * 512:(qi + 1) * 512], yst[hh][0:64, :], reads=[("yst", hh)], writes=[uq("yT_A")])
                        pend.extend([epi1, epi2])
                    it += 1
            flush()

            P.barrier()
            ar.off = B_base
            kzp = [ar.alloc([S], BF16) for _ in range(2)]
            qblk = ar.alloc([S], BF16)
            vsbc = ar.alloc([NT * 260], BF16)
            vsbc4 = vsbc.rearrange("p (t h c) -> p t h c", t=NT, h=4)
            pTd = [ar.alloc([1024], BF16) for _ in range(2)]
            osb = [[ar.alloc([512], F32) for _ in range(2)] for _ in range(2)]
            rden = [ar.alloc([512], F32) for _ in range(2)]
            bden = [ar.alloc([512], F32) for _ in range(2)]
            o1 = ar.alloc([512], F32)
            o2 = ar.alloc([512], F32)
            dsq = ar.alloc([512], F32)
            rs = ar.alloc([512], F32)
            rtmp = ar.alloc([512], F32)
            sgl = [[ar.alloc([512], BF16) for _ in range(2)] for _ in range(2)]
            yst = [ar.alloc([512], BF16) for _ in range(2)]
            scale_c = 32 ** -0.5
            AUX = 6
            it = 0
            pend = []

            def flush():
                while pend:
                    pend.pop(0)()

            for q4 in range(NT // 4):
                P.dma("sp", vsbc.rearrange("p (t c) -> p t c", t=NT)[:, q4 * 4:(q4 + 1) * 4, :],
                      vC_d[q4 * 512:(q4 + 1) * 512, :].rearrange("(t p) c -> p t c", p=128), reads=["vC_d"], writes=["vsbc"])
            for hb in range(2):
                P.dma("sp", qblk, qTC_d[hb * 128:(hb + 1) * 128, :], reads=["qTC_d"], writes=["qblk"])
                for c in range(2):
                    mset("pool", kzp[c], 0.0, [], [("kzp", c)])
                    for hh in range(2):
                        r0 = hh * 64 + c * 32
                        P.dma("sp", kzp[c][r0:r0 + 32, :], kTC_d[hb * 128 + r0:hb * 128 + r0 + 32, :],
                              reads=["kTC_d", ("kzp", c)], writes=[("kzp", c)])
                for qi in range(NCH):
                    s2 = it % 2
                    for hh in range(2):
                        h_ = 2 * hb + hh
                        P.dma("sp", sgl[s2][hh][0:64, :], sgC_d[h_ * 64:(h_ + 1) * 64, qi * 512:(qi + 1) * 512], reads=["scr_out"], writes=[("sgl", s2, hh)])
                    for c in range(2):

                        def qk(sb, c=c, qi=qi):
                            p_ = sb % 2
                            for hh in range(2):
                                b = 2 * p_ + hh
                                rows = slice(hh * 64, (hh + 1) * 64)
                                mm(bk(b), kzp[c][rows, sb * 128:(sb + 1) * 128], qblk[rows, qi * 512:(qi + 1) * 512], True, True,
                                   [("kzp", c), "qblk"], [("bank", b)])

                        qk(0)
                        if NT > 1:
                            qk(1)
                        for sb in range(NT):
                            p_ = sb % 2
                            for hh in range(2):
                                act(pTd[p_][:, hh * 512:(hh + 1) * 512], bk(2 * p_ + hh), AF.Exp, [("bank", 2 * p_ + hh)], [("pT", 2 * p_ + hh)], scale=scale_c)
                            if pend and sb % 3 == 1:
                                pend.pop(0)()
                            for hh in range(2):
                                if hh == 1 and sb + 2 < NT:
                                    qk(sb + 2)
                                mm(bk(4 + hh)[0:65, :], vsbc4[:, sb, 2 * hb + hh, :], pTd[p_][:, hh * 512:(hh + 1) * 512], sb == 0, sb == NT - 1,
                                   ["vsbc", ("pT", 2 * p_ + hh)], [("bank", 4 + hh)])
                                bstep()
                        if c == 0:
                            flush()
                        for hh in range(2):
                            cp("dve", osb[hh][c][0:65, :], bk(4 + hh)[0:65, :], [("bank", 4 + hh)], [("osb", hh, c)])

                    for hh in range(2):
                        h = 2 * hb + hh

                        def e1(hh=hh):
                            rcp(rden[0][64:65, :], osb[hh][0][64:65, :], [("osb", hh, 0)], [("rden", 0)])
                            rcp(rden[1][64:65, :], osb[hh][1][64:65, :], [("osb", hh, 1)], [("rden", 1)])
                            mm(bk(AUX)[0:64, :], ones_f[64:65, 0:64], rden[0][64:65, :], True, True, ["ones_f", ("rden", 0)], [("bank", AUX)])

                        def e2():
                            cp("dve", bden[0][0:64, :], bk(AUX)[0:64, :], [("bank", AUX)], [("bden", 0)])
                            mm(bk(AUX)[0:64, :], ones_f[64:65, 0:64], rden[1][64:65, :], True, True, ["ones_f", ("rden", 1)], [("bank", AUX)])

                        def e3(hh=hh):
                            cp("dve", bden[1][0:64, :], bk(AUX)[0:64, :], [("bank", AUX)], [("bden", 1)])
                            tt("dve", o1[0:64, :], osb[hh][0][0:64, :], bden[0][0:64, :], ALU.mult, [("osb", hh, 0), ("bden", 0)], ["o1"])
                            tt("dve", o2[0:64, :], osb[hh][1][0:64, :], bden[1][0:64, :], ALU.mult, [("osb", hh, 1), ("bden", 1)], ["o2"])
                            stt("dve", o1[0:64, :], o2[0:64, :], neglam[0:64, :], o1[0:64, :], ALU.mult, ALU.add, ["o1", "o2", "neglam"], ["o1"])
                            tt("pool", dsq[0:64, :], o1[0:64, :], o1[0:64, :], ALU.mult, ["o1"], ["dsq"])

                        def e4():
                            mm(bk(AUX)[0:64, :], ones_f[0:64, 0:64], dsq[0:64, :], True, True, ["ones_f", "dsq"], [("bank", AUX)])

                        def e5(h=h, hh=hh, qi=qi, s2=s2):
                            ts("dve", rtmp[0:64, :], bk(AUX)[0:64, :], 1.0 / 64, EPS, ALU.mult, ALU.add, [("bank", AUX)], ["rtmp"])
                            act(rtmp[0:64, :], rtmp[0:64, :], AF.Ln, ["rtmp"], ["rtmp"])
                            act(rs[0:64, :], rtmp[0:64, :], AF.Exp, ["rtmp"], ["rs"], scale=-0.5)
                            stt("dve", o2[0:64, :], o1[0:64, :], subg[0:64, :], rs[0:64, :], ALU.mult, ALU.mult, ["o1", "subg", "rs"], ["o2"])
                            tt("pool", yst[hh][0:64, :], o2[0:64, :], sgl[s2][hh][0:64, :], ALU.mult, ["o2", ("sgl", s2, hh)], [("yst", hh)])
                            P.dma("pool", yT_d[512 + h * 64:512 + (h + 1) * 64, qi * 512:(qi + 1) * 512], yst[hh][0:64, :], reads=[("yst", hh)], writes=[uq("yT_C")])
                        pend.extend([e1, e2, e3, e4, e5])
                    it += 1
            flush()

            for _ in bgen:
                pass
            tb_words = (W * 2 + 3) // 4
            tb_words = (tb_words + 15) // 16 * 16
            if NTB >= 3 and 2 * tb_words >= 4096:
                wo_pref = ar.t[0:128, tb_off0 + tb_words:tb_off0 + tb_words + 4096].bitcast(BF16)
                wp3 = wo_pref.rearrange("p (k c) -> p k c", k=8)
                for kc in range(8):
                    P.dma("pool", wp3[:, kc, :], wout_d[l, kc * 128:(kc + 1) * 128, :],
                          writes=[("wo", kc), ("tbuf", 1), ("tbuf", 2)])
            ykeys = [("ypr", g) for g in range(NG)]
            it = 0
            for n in range(NCH):
                for hh in range(2):
                    b = 4 + (it % 2)
                    for ti in range(4):
                        mm(bk(b)[:, ti * 128:(ti + 1) * 128], ypr3[:, 4 * n + ti, hh * 128:(hh + 1) * 128], jmat, True, True,
                           ykeys + ["jmat"], [("bank", b)], skip_group_check=True)
                    P.dma("sp", zcc[it % 2], zT_d[hh * 128:(hh + 1) * 128, n * 512:(n + 1) * 512], reads=["scr_out"], writes=[("zcc", it % 2)])
                    P.dma("sp", xgc[it % 2], xgB_d[hh * 128:(hh + 1) * 128, n * 512:(n + 1) * 512], reads=["scr_out"], writes=[("xgc", it % 2)])
                    stt("dve", et, zcc[it % 2], hybias[:, hh:hh + 1], bk(b), ALU.mult, ALU.add, [("zcc", it % 2), "hybias", ("bank", b)], ["et"])
                    tt("dve", ystB[it % 2], et, xgc[it % 2], ALU.mult, ["et", ("xgc", it % 2)], [("ystB", it % 2)])
                    P.dma("pool", yT_d[256 + hh * 128:256 + (hh + 1) * 128, n * 512:(n + 1) * 512], ystB[it % 2], reads=[("ystB", it % 2)], writes=[uq("yT_B")])
                    it += 1

        if "O" in stages:
            P.barrier()
            ar.off = ppos
            if wo_pref is None:
                wo = ar.alloc([8 * 1024], BF16)
            else:
                wo = wo_pref
            wo3 = wo.rearrange("p (k c) -> p k c", k=8)
            yc = [ar.alloc([8 * 512], BF16) for _ in range(2)]
            xt = [ar.alloc([1024], F32) for _ in range(3)]
            ot = [ar.alloc([1024], F32) for _ in range(2)]
            rt_ = [ar.alloc([1024], F32) for _ in range(2)]
            assert wo_pref is None or ar.off <= tb_off0 + tb_words, "stage O buffers overlap the prefetched w_out"
            if wo_pref is None:
                for kc in range(8):
                    P.dma("pool", wo3[:, kc, :], wout_d[l, kc * 128:(kc + 1) * 128, :], writes=[("wo", kc)])
            wok = [("wo", kc) for kc in range(8)]
            for n in range(NCH):
                y3 = yc[n % 2].rearrange("p (k t) -> p k t", k=8)
                P.dma("sp", y3, yT_d[:, n * 512:(n + 1) * 512].rearrange("(k p) t -> p k t", p=128),
                      reads=["yT_A", "yT_B", "yT_C", "scr_out"], writes=[("yc", n % 2)])
                for ti in range(4):
                    T = 4 * n + ti
                    xs = T % 3
                    P.dma("sp", xt[xs], xsrc[T * 128:(T + 1) * 128, :], reads=[("xout", l - 1, T)], writes=[("xt", xs)])
                    for h in range(2):
                        b = (2 * T + h) % 4
                        for kc in range(8):
                            mm(bk(b), y3[:, kc, ti * 128:(ti + 1) * 128], wo3[:, kc, h * 512:(h + 1) * 512], kc == 0, kc == 7,
                               [("yc", n % 2)] + wok, [("bank", b)])
                        tt("dve", ot[T % 2][:, h * 512:(h + 1) * 512], bk(b), G_bc[:, h * 512:(h + 1) * 512], ALU.mult,
                           [("bank", b), ("G_bc", h)], [("ot", T % 2, h)])
                    tt("pool", rt_[T % 2], ot[T % 2], xt[xs], ALU.add, [("ot", T % 2, 0), ("ot", T % 2, 1), ("xt", xs)], [("rt", T % 2)])
                    P.dma("pool", xdst[T * 128:(T + 1) * 128, :], rt_[T % 2], reads=[("rt", T % 2)], writes=[("xout", l, T)])

    P.barrier()
    P.op("sp", lambda e: e.nop(), [], [])
    P.emit()
    return nc


def host_tables(S):
    f32 = np.float32
    L = S
    t = np.arange(S)
    row = (t // 64).astype(f32)
    col = (t % 64).astype(f32)
    inv = (np.float32(10000.0) ** (-np.arange(0, 32, 2, dtype=f32) / np.float32(32))).astype(f32)

    def cs(pos):
        ang = pos.astype(f32)[:, None] * inv[None, :]
        return np.cos(ang).astype(f32), np.sin(ang).astype(f32)
    cr, sr = cs(row)
    cc, sc = cs(col)
    c1, s1 = cs(t.astype(f32))
    rope = np.concatenate([cr, cr, cc, cc, -sr, sr, -sc, sc, c1, c1, -s1, s1], axis=1).astype(f32)
    tt_ = np.linspace(0.0, 1.0, L, dtype=f32)[:, None]
    w = (np.float32(2.0 * math.pi) * np.arange(L, dtype=f32)[:, None] / np.float32(L)).astype(f32)
    f = np.linspace(1e-4, 15, 16, dtype=f32)[None, :]
    emb = np.concatenate([tt_, np.cos(f * w).astype(f32), -np.sin(f * w).astype(f32)], axis=-1).astype(f32)
    d = L - 1 - np.arange(2 * L)
    pos = np.abs(d)
    pos[pos >= L] = 0
    emb_tab = np.ascontiguousarray(emb[pos].T)
    tpos = tt_[pos, 0][None, :].astype(f32)
    max_decay = math.log(1e-2) / 0.3
    min_decay = math.log(1e-2) / 1.5
    deltas = np.linspace(min_decay, max_decay, 256, dtype=f32)
    negdelta = np.ascontiguousarray((-np.abs(deltas)).reshape(2, 128).T).astype(f32)
    eye = np.eye(128, dtype=f32)
    jm = np.ascontiguousarray(eye[::-1])
    return dict(rope_tab=rope, emb_tab=emb_tab, tpos_tab=tpos, negdelta=negdelta, eye_tab=eye, jmat_tab=jm)


def host_layout(inp, b, NL):
    f32 = np.float32
    g = lambda k: np.asarray(inp[k], dtype=f32)
    m = {}
    m["x"] = np.ascontiguousarray(g("x")[b])
    m["cT"] = np.ascontiguousarray(g("c")[b].reshape(8, 128).T)
    m["norm_g"] = g("norm_g")[:NL, None, :]
    m["w_ada"] = g("w_ada")[:NL]
    m["b_ada"] = g("b_ada")[:NL, None, :]
    m["w_in"] = g("w_in")[:NL]
    m["w_out"] = g("w_out")[:NL]
    def swp(a, hd):
        a4 = a.reshape(a.shape[0], hd // 32, 2, 16)
        return a4[:, :, ::-1, :].reshape(a.shape[0], hd)
    aq, ak, cq, ck = g("a_qn")[:NL], g("a_kn")[:NL], g("c_qn")[:NL], g("c_kn")[:NL]
    m["gnrow"] = np.concatenate([np.tile(aq, (1, 4)), np.tile(ak, (1, 2)), np.tile(cq, (1, 8)), np.tile(ck, (1, 8)),
                                 np.tile(swp(aq, 64), (1, 4)), np.tile(swp(ak, 64), (1, 2)),
                                 np.tile(swp(cq, 32), (1, 8)), np.tile(swp(ck, 32), (1, 8))], axis=1)[:, None, :]
    m["hy_conv_wT"] = np.ascontiguousarray(g("hy_conv_w")[:NL].reshape(NL, 3, 6, 128).transpose(0, 3, 2, 1))
    m["hy_conv_bT"] = np.ascontiguousarray(g("hy_conv_b")[:NL].reshape(NL, 6, 128).transpose(0, 2, 1))
    m["hy_w1"] = g("hy_w1")[:NL]
    m["hy_b1T"] = g("hy_b1")[:NL, :, None]
    m["hy_freqT"] = g("hy_freq")[:NL, :, None]
    m["hy_w2"] = g("hy_w2")[:NL]
    m["hy_b2T"] = g("hy_b2")[:NL, :, None]
    m["hy_w3"] = g("hy_w3")[:NL]
    m["hy_biasT"] = np.ascontiguousarray(g("hy_bias")[:NL].reshape(NL, 2, 128).transpose(0, 2, 1))
    m["lamrow"] = np.concatenate([g("lam_q1")[:NL], g("lam_k1")[:NL], g("lam_q2")[:NL], g("lam_k2")[:NL]], axis=1)[:, None, :]
    m["c_sublnT"] = g("c_subln")[:NL, :, None]
    m["sc_conv_wT"] = np.ascontiguousarray(g("sc_conv_w")[:NL].reshape(NL, 3, 2, 128).transpose(0, 3, 2, 1))
    return {k: np.ascontiguousarray(v, dtype=f32) for k, v in m.items()}


_CACHE = {}


def kernel(**inputs):
    x = np.asarray(inputs["x"])
    B, S, _ = x.shape
    NL = np.asarray(inputs["w_in"]).shape[0]
    key = (S, NL)
    if key not in _CACHE:
        _CACHE[key] = (build(S, NL), host_tables(S))
    nc, tabs = _CACHE[key]
    in_maps = []
    for b in range(B):
        m = host_layout(inputs, b, NL)
        m.update(tabs)
        in_maps.append(m)
    res = run_bass_kernel_spmd(nc, in_maps, core_ids=list(range(B)))
    return np.stack([np.asarray(r["out"]) for r in res.results], axis=0).astype(np.float32)
```

```python
import math
import contextlib
import numpy as np
import concourse.bass as bass
import concourse.mybir as mybir
from concourse.bass import AP
from concourse.bass_utils import run_bass_kernel_spmd

F32 = mybir.dt.float32
BF16 = mybir.dt.bfloat16
AF = mybir.ActivationFunctionType
ALU = mybir.AluOpType
AX = mybir.AxisListType

D_MODEL = 1024
D_IN = 3840
EPS = 1e-6
MAGIC = 12582912.0
TWO_PI = 2.0 * math.pi

O_AQ, O_AK, O_AV, O_AG = 0, 256, 384, 512
O_BP, O_BG = 768, 1536
O_CQ, O_CK, O_CV, O_CG = 1792, 2048, 2304, 2560
O_DP, O_DG = 2816, 3584


class Op:
    __slots__ = ("eng", "fn", "deps", "signal", "sigval", "is_dma", "dsem", "dval", "idx", "predma")

    def __init__(self, eng, fn, is_dma=False):
        self.eng = eng
        self.fn = fn
        self.deps = []
        self.signal = False
        self.sigval = None
        self.is_dma = is_dma
        self.dsem = None
        self.dval = None
        self.predma = None
        self.idx = None


ENGS = ("pe", "act", "dve", "pool", "sp")
QUEUES = ("sp", "act", "pool")


class Prog:
    NPOOL = 8

    def __init__(self, nc):
        self.nc = nc
        self.ops = {e: [] for e in ENGS}
        self.last_w = {}
        self.readers = {}
        self.dma_hist = {q: [] for q in QUEUES}
        self.bar_deps = []
        self.bar_gen = 0
        self.eng_gen = {e: 0 for e in ENGS}

    def _add_dep(self, op, dep):
        if dep is None or dep is op:
            return
        if dep.eng == "pe" and op.eng == "pe" and not dep.is_dma and not op.is_dma:
            return
        op.deps.append(dep)
        if not dep.is_dma:
            dep.signal = True

    def op(self, eng, fn, reads=(), writes=(), is_dma=False):
        o = Op(eng, fn, is_dma)
        bankr = [k for k in reads if isinstance(k, tuple) and k[0] == "bank"]
        if bankr:
            reads = [k for k in reads if k not in bankr]
            writes = list(writes) + [k for k in bankr if k not in writes]
        if self.eng_gen[eng] != self.bar_gen:
            self.eng_gen[eng] = self.bar_gen
            for d in self.bar_deps:
                if d.eng == eng and not d.is_dma and not is_dma:
                    continue
                self._add_dep(o, d)
        for k in reads:
            self._add_dep(o, self.last_w.get(k))
        for k in writes:
            self._add_dep(o, self.last_w.get(k))
            for r in self.readers.get(k, ()):
                if r.eng == eng and not r.is_dma and not is_dma:
                    continue
                self._add_dep(o, r)
        for k in reads:
            lst = self.readers.setdefault(k, [])
            if not is_dma:
                lst[:] = [r for r in lst if r.is_dma or r.eng != eng]
            lst.append(o)
        for k in writes:
            self.last_w[k] = o
            self.readers[k] = []
        if is_dma:
            hist = self.dma_hist[eng]
            j = len(hist)
            o.idx = j
            if j >= self.NPOOL:
                o.predma = hist[j - self.NPOOL]
            hist.append(o)
        self.ops[eng].append(o)
        return o

    def dma(self, q, out, in_, reads=(), writes=(), **kw):
        def fn(e, out=out, in_=in_, kw=kw):
            return e.dma_start(out=out, in_=in_, **kw)
        return self.op(q, fn, reads, writes, is_dma=True)

    def barrier(self):
        lasts = []
        for e in ENGS:
            for o in reversed(self.ops[e]):
                if not o.is_dma:
                    lasts.append(o)
                    break
        for q in QUEUES:
            lasts.extend(self.dma_hist[q][-self.NPOOL:])
        self.bar_deps = lasts
        self.bar_gen += 1

    def emit(self):
        nc = self.nc
        for e in ENGS:
            c = 0
            for o in self.ops[e]:
                if o.is_dma:
                    continue
                if o.signal:
                    c += 1
                    o.sigval = c
        with contextlib.ExitStack() as st:
            esem = {e: st.enter_context(nc.semaphore(f"s_{e}")) for e in ENGS}
            dsem = {q: [st.enter_context(nc.semaphore(f"d_{q}{i}")) for i in range(self.NPOOL)]
                    for q in QUEUES}
            for q in QUEUES:
                for o in self.dma_hist[q]:
                    o.dsem = dsem[q][o.idx % self.NPOOL]
                    o.dval = 16 * (o.idx // self.NPOOL + 1)
            block = st.enter_context(nc.Block())

            def run(ename, eng):
                seen = {}
                for o in self.ops[ename]:
                    waits = {}
                    deps = list(o.deps)
                    if o.predma is not None:
                        deps.append(o.predma)
                    for d in deps:
                        if d.is_dma:
                            s, v = d.dsem, d.dval
                        else:
                            s, v = esem[d.eng], d.sigval
                        key = id(s)
                        if seen.get(key, 0) >= v:
                            continue
                        if key not in waits or waits[key][1] < v:
                            waits[key] = (s, v)
                    for key, (s, v) in waits.items():
                        eng.wait_ge(s, v)
                        seen[key] = v
                    ins = o.fn(eng)
                    if o.is_dma:
                        ins.then_inc(o.dsem, 16)
                    elif o.signal:
                        ins.then_inc(esem[ename], 1)

            @block.sync
            def _(sync):
                run("sp", sync)

            @block.scalar
            def _(scalar):
                run("act", scalar)

            @block.vector
            def _(vector):
                run("dve", vector)

            @block.gpsimd
            def _(gpsimd):
                run("pool", gpsimd)

            @block.tensor
            def _(tensor):
                run("pe", tensor)


def cust(base, delta, dims):
    pd = list(base.ap[0])
    return AP(base.tensor, base.offset + delta, [pd] + [list(d) for d in dims])


class Arena:
    def __init__(self, nc, words):
        self.t = nc.alloc_sbuf_tensor("arena", [128, words], F32)
        self.words = words
        self.off = 0
        self.marks = []

    def alloc(self, shape, dt, parts=128):
        esz = 4 if dt == F32 else 2
        n = int(np.prod(shape))
        w = (n * esz + 3) // 4
        w = (w + 15) // 16 * 16
        assert self.off + w <= self.words, f"SBUF arena overflow: {self.off}+{w} > {self.words}"
        v = self.t[0:parts, self.off:self.off + (n * esz + 3) // 4]
        if dt != F32:
            v = v.bitcast(dt)
        self.off += w
        return v[:, 0:n]

    def mark(self):
        self.marks.append(self.off)

    def release(self):
        self.off = self.marks.pop()


def build(S, NL, debug=False, stages="MF1ACBO"):
    NT = S // 128
    NCH = S // 512
    L2 = 2 * S
    NF = L2 // 512
    W = L2 - 128
    CPB = 512 // NT
    assert CPB >= 1 and 256 % CPB == 0

    nc = bass.Bass("TRN2", target_bir_lowering=False)
    P = Prog(nc)
    scratch_kind = "ExternalOutput" if debug else "Internal"

    def din(name, shape, dt=F32):
        return nc.dram_tensor(name, list(shape), dt, kind="ExternalInput")

    def dscr(name, shape, dt=BF16):
        return nc.dram_tensor(name, list(shape), dt, kind=scratch_kind)

    x_in = din("x", [S, D_MODEL])
    cT_d = din("cT", [128, 8])
    normg_d = din("norm_g", [NL, 1, D_MODEL])
    wada_d = din("w_ada", [NL, D_MODEL, 3 * D_MODEL])
    bada_d = din("b_ada", [NL, 1, 3 * D_MODEL])
    win_d = din("w_in", [NL, D_MODEL, D_IN])
    wout_d = din("w_out", [NL, D_MODEL, D_MODEL])
    gnrow_d = din("gnrow", [NL, 1, 1792])
    hycw_d = din("hy_conv_wT", [NL, 128, 6, 3])
    hycb_d = din("hy_conv_bT", [NL, 128, 6])
    hyw1_d = din("hy_w1", [NL, 33, 64])
    hyb1_d = din("hy_b1T", [NL, 64, 1])
    hyfr_d = din("hy_freqT", [NL, 64, 1])
    hyw2_d = din("hy_w2", [NL, 64, 64])
    hyb2_d = din("hy_b2T", [NL, 64, 1])
    hyw3_d = din("hy_w3", [NL, 64, 512])
    hybias_d = din("hy_biasT", [NL, 128, 2])
    lam_d = din("lamrow", [NL, 1, 128])
    subln_d = din("c_sublnT", [NL, 64, 1])
    sccw_d = din("sc_conv_wT", [NL, 128, 2, 3])
    rope_d = din("rope_tab", [S, 192])
    emb_d = din("emb_tab", [33, L2])
    tpos_d = din("tpos_tab", [1, L2])
    ndel_d = din("negdelta", [128, 2])
    eye_d = din("eye_tab", [128, 128])
    jmat_d = din("jmat_tab", [128, 128])

    out_d = nc.dram_tensor("out", [S, D_MODEL], F32, kind="ExternalOutput")
    xres_d = dscr("xres", [S, D_MODEL], F32)
    gflt_d = dscr("gflt", [256, L2])
    qTA_d = dscr("qTA", [256, S])
    kTA_d = dscr("kTA", [256, S])
    vA_d = dscr("vA", [S, 130])
    qTC_d = dscr("qTC", [256, S])
    kTC_d = dscr("kTC", [256, S])
    vC_d = dscr("vC", [S, 260])
    sgA_d = dscr("sgA", [256, S])
    sgC_d = dscr("sgC", [256, S])
    xgB_d = dscr("xgB", [256, S])
    zT_d = dscr("zT", [256, S])
    yT_d = dscr("yT", [1024, S])

    ar = Arena(nc, 51000)
    dbanks = [nc.alloc_psum_tensor(f"dbank{i}", [128, 1024], F32) for i in range(4)]

    def bk(i):
        return dbanks[i // 2][:, (i % 2) * 512:(i % 2 + 1) * 512]

    def bkb(i):
        return bk(i).bitcast(BF16)

    def bk2(i):
        return dbanks[i // 2][:, :]

    ident = ar.alloc([128], BF16)
    jmat = ar.alloc([128], BF16)
    ones_f = ar.alloc([128], F32)
    ones_b = ar.alloc([128], BF16)
    A_bc = ar.alloc([1024], F32)
    SH_bc = ar.alloc([1024], F32)
    G_bc = ar.alloc([1024], F32)
    GN_bc = ar.alloc([1792], F32)
    sc_col = ar.alloc([8], F32)
    cT_sb = ar.alloc([8], F32)
    hycw = ar.alloc([18], F32)
    hycb = ar.alloc([6], F32)
    sccw = ar.alloc([6], F32)
    hybias = ar.alloc([2], F32)
    ndel = ar.alloc([2], F32)
    subg = ar.alloc([1], F32)
    neglam = ar.alloc([1], F32)
    ppos = ar.off

    _uq = [0]

    def uq(name):
        _uq[0] += 1
        return (name, _uq[0])

    def mm(out, lhsT, rhs, start, stop, reads, writes, **kw):
        return P.op("pe", lambda e: e.matmul(out, lhsT=lhsT, rhs=rhs, start=start, stop=stop, **kw), reads, writes)

    def tr(out, in_, reads, writes):
        return P.op("pe", lambda e: e.transpose(out=out, in_=in_, identity=ident), reads + ["ident"], writes)

    def act(out, in_, func, reads, writes, **kw):
        return P.op("act", lambda e: e.activation(out=out, in_=in_, func=func, **kw), reads, writes)

    def tt(eng, out, in0, in1, op, reads, writes):
        return P.op(eng, lambda e: e.tensor_tensor(out=out, in0=in0, in1=in1, op=op), reads, writes)

    def ts(eng, out, in0, s1, s2, op0, op1, reads, writes):
        return P.op(eng, lambda e: e.tensor_scalar(out=out, in0=in0, scalar1=s1, scalar2=s2, op0=op0, op1=op1), reads, writes)

    def stt(eng, out, in0, scalar, in1, op0, op1, reads, writes):
        return P.op(eng, lambda e: e.scalar_tensor_tensor(out=out, in0=in0, scalar=scalar, in1=in1, op0=op0, op1=op1), reads, writes)

    def cp(eng, out, in_, reads, writes):
        if eng == "act":
            return P.op("act", lambda e: e.copy(out=out, in_=in_), reads, writes)
        return P.op(eng, lambda e: e.tensor_copy(out=out, in_=in_), reads, writes)

    def rcp(out, in_, reads, writes):
        return P.op("dve", lambda e: e.reciprocal(out=out, in_=in_), reads, writes)

    def mset(eng, out, val, reads, writes):
        return P.op(eng, lambda e: e.memset(out, val), reads, writes)

    def red(out, in_, reads, writes, absv=False):
        if absv:
            return P.op("dve", lambda e: e.tensor_reduce(out=out, in_=in_, axis=AX.X, op=ALU.add, apply_absolute_value=True), reads, writes)
        return P.op("dve", lambda e: e.tensor_reduce(out=out, in_=in_, axis=AX.X, op=ALU.add), reads, writes)

    def rsqrt_chain(dst, src, scale, n_key, reads, tmp):
        ts("dve", tmp, src, scale, EPS, ALU.mult, ALU.add, reads, [n_key + ("t",)])
        act(tmp, tmp, AF.Sqrt, [n_key + ("t",)], [n_key + ("t",)])
        rcp(dst, tmp, [n_key + ("t",)], [n_key])

    P.dma("pool", ident, eye_d.ap(), writes=["ident"])
    P.dma("pool", jmat, jmat_d.ap(), writes=["jmat"])
    mset("dve", ones_f, 1.0, [], ["ones_f"])
    mset("dve", ones_b, 1.0, [], ["ones_b"])
    P.dma("sp", cT_sb, cT_d.ap(), writes=["cT"])
    P.dma("sp", ndel, ndel_d.ap(), writes=["ndel"])
    act(sc_col, cT_sb, AF.Silu, ["cT"], ["sc"])

    for l in range(NL):
        xsrc = x_in if l == 0 else xres_d
        xdst = out_d if l == NL - 1 else xres_d
        lambda_init = 0.8 - 0.6 * math.exp(-0.3 * l)

        if "M" in stages:
            P.barrier()
            ar.off = ppos
            wada = [ar.alloc([3072], F32) for _ in range(2)]
            row = ar.alloc([3072], F32)
            brow = ar.alloc([3072], F32)
            grow = ar.alloc([1024], F32)
            arow = ar.alloc([1024], F32)
            gnrow = ar.alloc([1792], F32)
            lamrow = ar.alloc([128], F32)
            lamt = ar.alloc([8], F32)
            mset("dve", lamt, 0.0, [], ["lamrow_init"])

            P.dma("sp", brow[0:1, :], bada_d[l], writes=["brow"])
            P.dma("sp", grow[0:1, :], normg_d[l], writes=["grow"])
            P.dma("sp", gnrow[0:1, :], gnrow_d[l], writes=["gnrow"])
            P.dma("sp", lamrow[0:1, :], lam_d[l], writes=["lamrow"])
            P.dma("sp", hycw, hycw_d[l].rearrange("p a b -> p (a b)"), writes=["hycw"])
            P.dma("sp", hycb, hycb_d[l], writes=["hycb"])
            P.dma("sp", sccw, sccw_d[l].rearrange("p a b -> p (a b)"), writes=["sccw"])
            P.dma("sp", hybias, hybias_d[l], writes=["hybias"])
            P.dma("sp", subg[0:64, :], subln_d[l], writes=["subg"])
            ts("dve", subg[0:64, :], subg[0:64, :], 1.0 - lambda_init, None, ALU.mult, ALU.bypass, ["subg"], ["subg"])

            for kc in range(8):
                P.dma("sp", wada[kc % 2], wada_d[l, kc * 128:(kc + 1) * 128, :], writes=[("wada", kc % 2)])
                for n in range(6):
                    mm(bk(n)[0:1, :], sc_col[:, kc:kc + 1], wada[kc % 2][:, n * 512:(n + 1) * 512],
                       kc == 0, kc == 7, ["sc", ("wada", kc % 2)], [("bank", n)])
            for n in range(6):
                tt("dve", row[0:1, n * 512:(n + 1) * 512], bk(n)[0:1, :], brow[0:1, n * 512:(n + 1) * 512], ALU.add,
                   [("bank", n), "brow"], [("row", n)])
            stt("dve", arow[0:1, :], row[0:1, 1024:2048], 1.0, grow[0:1, :], ALU.add, ALU.mult,
                [("row", 2), ("row", 3), "grow"], ["arow"])
            bsrc = [(arow, 0, A_bc, "A_bc", ["arow"]), (row, 0, SH_bc, "SH_bc", [("row", 0), ("row", 1)]),
                    (row, 2048, G_bc, "G_bc", [("row", 4), ("row", 5)])]
            bi = 6
            for (src, so, dst, key, rd) in bsrc:
                for h in range(2):
                    b = 6 + (bi % 2)
                    bi += 1
                    mm(bk(b), ones_f[0:1, :], src[0:1, so + h * 512: so + (h + 1) * 512], True, True,
                       ["ones_f"] + rd, [("bank", b)])
                    cp("act", dst[:, h * 512:(h + 1) * 512], bk(b), [("bank", b)], [(key, h)])
            for (lo, hi) in ((0, 512), (512, 1024), (1024, 1536), (1536, 1792)):
                b = 6 + (bi % 2)
                bi += 1
                mm(bk(b)[:, 0:hi - lo], ones_f[0:1, :], gnrow[0:1, lo:hi], True, True, ["ones_f", "gnrow"], [("bank", b)])
                cp("act", GN_bc[:, lo:hi], bk(b)[:, 0:hi - lo], [("bank", b)], ["GN_bc"])
            lr = lamrow[0:1, :].rearrange("p (a b) -> p a b", b=32)
            prod = ar.alloc([64], F32)
            tt("dve", prod[0:1, 0:32], lr[:, 0, :], lr[:, 1, :], ALU.mult, ["lamrow"], ["prod0"])
            tt("dve", prod[0:1, 32:64], lr[:, 2, :], lr[:, 3, :], ALU.mult, ["lamrow"], ["prod1"])
            red(lamt[0:1, 2:4], prod[0:1, :].rearrange("p (a b) -> p a b", b=32), ["prod0", "prod1"], ["lamsum"])
            act(lamt[0:1, 4:6], lamt[0:1, 2:4], AF.Exp, ["lamsum"], ["lamexp"])
            stt("dve", lamt[0:1, 6:7], lamt[0:1, 5:6], -lambda_init, lamt[0:1, 4:5], ALU.add, ALU.subtract,
                ["lamexp"], ["neglam_row"])
            mm(bk(6)[0:64, 0:2], ones_f[0:1, 0:64], lamt[0:1, 6:8], True, True, ["ones_f", "neglam_row"], [("bank", 6)])
            cp("act", neglam[0:64, :], bk(6)[0:64, 0:1], [("bank", 6)], ["neglam"])

        if "F" in stages:
            P.barrier()
            ar.off = ppos
            w1 = ar.alloc([64], F32)
            w2 = ar.alloc([64], F32)
            w3 = ar.alloc([512], F32)
            b1c = ar.alloc([1], F32)
            frc = ar.alloc([1], F32)
            b2c = ar.alloc([1], F32)
            tposr = ar.alloc([L2], F32)
            embc = [ar.alloc([512], F32) for _ in range(2)]
            argb = [ar.alloc([512], F32) for _ in range(2)]
            tmpb = [ar.alloc([512], F32) for _ in range(2)]
            h1s = [ar.alloc([512], F32) for _ in range(2)]
            h2s = ar.alloc([L2], F32)
            winb = [ar.alloc([512], F32) for _ in range(2)]
            kbuf2 = [ar.alloc([L2], F32) for _ in range(2)]
            gb = ar.alloc([L2], BF16)
            nrm = ar.alloc([4], F32)

            P.dma("sp", w1[0:33, :], hyw1_d[l], writes=["w1"])
            P.dma("sp", w2[0:64, :], hyw2_d[l], writes=["w2"])
            P.dma("sp", w3[0:64, :], hyw3_d[l], writes=["w3"])
            P.dma("sp", b1c[0:64, :], hyb1_d[l], writes=["b1c"])
            P.dma("sp", frc[0:64, :], hyfr_d[l], writes=["frc"])
            P.dma("sp", b2c[0:64, :], hyb2_d[l], writes=["b2c"])
            P.dma("sp", tposr[0:1, :], tpos_d.ap(), writes=["tposr"])

            def sin_layer(ps, bcol, dst, ci, tag, wkey):
                a = argb[ci % 2][0:64, :]
                t = tmpb[ci % 2][0:64, :]
                ka, kt = ("argb", ci % 2), ("tmpb", ci % 2)
                ts("dve", a, ps, bcol[0:64, :], frc[0:64, :], ALU.add, ALU.mult, [tag, "b1c", "b2c", "frc"], [ka])
                ts("dve", t, a, 1.0 / TWO_PI, MAGIC, ALU.mult, ALU.add, [ka], [kt])
                ts("dve", t, t, -MAGIC, -TWO_PI, ALU.add, ALU.mult, [kt], [kt])
                tt("dve", a, a, t, ALU.add, [ka, kt], [ka])
                ts("dve", a, a, 3.1415925, -3.1415925, ALU.min, ALU.max, [ka], [ka])
                act(dst, a, AF.Sin, [ka], [wkey])

            for c0_ in range(0, NF, 2):
                wave = [ci for ci in (c0_, c0_ + 1) if ci < NF]
                for ci in wave:
                    e = embc[ci % 2]
                    P.dma("sp", e[0:33, :], emb_d[:, ci * 512:(ci + 1) * 512], writes=[("embc", ci % 2)])
                    b = ci % 2
                    mm(bk(b)[0:64, :], w1[0:33, :], e[0:33, :], True, True, ["w1", ("embc", ci % 2)], [("bank", b)])
                for ci in wave:
                    b = ci % 2
                    sin_layer(bk(b)[0:64, :], b1c, h1s[ci % 2][0:64, :], ci, ("bank", b), ("h1s", ci % 2))
                for ci in wave:
                    b = ci % 2
                    mm(bk(2 + b)[0:64, :], w2[0:64, :], h1s[ci % 2][0:64, :], True, True, ["w2", ("h1s", ci % 2)], [("bank", 2 + b)])
                for ci in wave:
                    b = ci % 2
                    sin_layer(bk(2 + b)[0:64, :], b2c, h2s[0:64, ci * 512:(ci + 1) * 512], ci, ("bank", 2 + b), ("h2s", ci))

            for ci in range(NF):
                fwd = (ci * 512) < S
                b2 = 6 + ci % 2
                mm(bk(b2), ones_f[0:1, :], tposr[0:1, ci * 512:(ci + 1) * 512], True, True,
                   ["ones_f", "tposr"], [("bank", b2)])
                for hh in range(2):
                    wc = (0 if fwd else 256) + hh * 128
                    b = 4 + hh
                    mm(bk(b), w3[0:64, wc:wc + 128], h2s[0:64, ci * 512:(ci + 1) * 512], True, True,
                       ["w3", ("h2s", ci)], [("bank", b)])
                    wv = winb[hh]
                    act(wv, bk(b2), AF.Exp, [("bank", b2), "ndel"], [("winb", hh)], scale=ndel[:, hh:hh + 1])
                    stt("dve", kbuf2[hh][:, ci * 512:(ci + 1) * 512], wv, 0.05, bk(b), ALU.add, ALU.mult,
                        [("winb", hh), ("bank", b)], [("kbuf", hh, ci)])
            for hh in range(2):
                kb_ = kbuf2[hh]
                mset("dve", kb_[:, L2 - 1:L2], 0.0, [("kbuf", hh, NF - 1)], [("kbuf", hh, NF - 1)])
                kall = [("kbuf", hh, ci) for ci in range(NF)]
                red(nrm[:, 2 * hh:2 * hh + 1], kb_, kall, [("nrm0", hh)], absv=True)
                rcp(nrm[:, 2 * hh + 1:2 * hh + 2], nrm[:, 2 * hh:2 * hh + 1], [("nrm0", hh)], [("nrm1", hh)])
                ts("dve", gb, kb_, nrm[:, 2 * hh + 1:2 * hh + 2], None, ALU.mult, ALU.bypass, kall + [("nrm1", hh)], ["gb"])
                P.dma("sp", gflt_d[hh * 128:(hh + 1) * 128, :], gb, reads=["gb"], writes=[uq("gflt")])

        if "1" in stages:
            P.barrier()
            ar.off = ppos
            winb16 = ar.alloc([8 * D_IN], BF16)
            win3 = winb16.rearrange("p (k c) -> p k c", k=8)
            xt = [ar.alloc([1024], F32) for _ in range(4)]
            junk = ar.alloc([1024], BF16)
            hb = [ar.alloc([1024], BF16) for _ in range(4)]
            hT = [ar.alloc([8 * 512], BF16) for _ in range(3)]
            hT3 = [h.rearrange("p (k t) -> p k t", k=8) for h in hT]
            st8 = ar.alloc([16], F32)
            ropet = [ar.alloc([192], F32) for _ in range(2)]
            GW = {"A": 384, "C": 512}
            bufS = {g_: [ar.alloc([GW[g_]], F32) for _ in range(2)] for g_ in "AC"}
            bufQ = {g_: [ar.alloc([GW[g_]], F32) for _ in range(2)] for g_ in "AC"}
            stat = {g_: [ar.alloc([48], F32) for _ in range(2)] for g_ in "AC"}
            tabC = {g_: ar.alloc([GW[g_]], F32) for g_ in "AC"}
            tabS = {g_: ar.alloc([GW[g_]], F32) for g_ in "AC"}
            qkb = {g_: [ar.alloc([GW[g_]], BF16) for _ in range(2)] for g_ in "AC"}
            qkAT = [ar.alloc([3 * 512], BF16) for _ in range(2)]
            qkCT = [ar.alloc([4 * 512], BF16) for _ in range(2)]
            vAs = [ar.alloc([4 * 130], BF16) for _ in range(2)]
            vCs = [ar.alloc([4 * 260], BF16) for _ in range(2)]
            pre = [ar.alloc([514], F32) for _ in range(4)]
            cva = ar.alloc([512], F32)
            cvb = ar.alloc([512], F32)
            cvc = ar.alloc([512], F32)
            sgt = ar.alloc([512], F32)
            fst = [ar.alloc([512], BF16) for _ in range(4)]
            fst_i = [0]

            for kc in range(8):
                for hcol in range(2):
                    P.dma("pool", win3[:, kc, hcol * 1920:(hcol + 1) * 1920],
                          win_d[l, kc * 128:(kc + 1) * 128, hcol * 1920:(hcol + 1) * 1920], writes=[("win", kc, hcol)])
            win_keys = [("win", kc, hcol) for kc in range(8) for hcol in range(2)]
            for s_ in range(2):
                va = vAs[s_].rearrange("p (t h c) -> p t h c", t=4, h=2)
                vc = vCs[s_].rearrange("p (t h c) -> p t h c", t=4, h=4)
                mset("pool", va[:, :, :, 64:65], 1.0, [], [("vAs1", s_)])
                mset("pool", vc[:, :, :, 64:65], 1.0, [], [("vCs1", s_)])

            def Hcompute(n):
                for ti in range(4):
                    T = 4 * n + ti
                    xs = T % 4
                    P.dma("sp", xt[xs], xsrc[T * 128:(T + 1) * 128, :], reads=[("xout", l - 1, T)], writes=[("xt", xs)])
                    act(junk, xt[xs], AF.Square, [("xt", xs)], ["junk", ("ss", xs)], accum_out=st8[:, xs:xs + 1])
                    ts("dve", st8[:, 4 + xs:5 + xs], st8[:, xs:xs + 1], 1.0 / D_MODEL, EPS, ALU.mult, ALU.add, [("ss", xs)], [("ms", xs)])
                    act(st8[:, 8 + xs:9 + xs], st8[:, 4 + xs:5 + xs], AF.Ln, [("ms", xs)], [("lnms", xs)])
                    act(st8[:, 12 + xs:13 + xs], st8[:, 8 + xs:9 + xs], AF.Exp, [("lnms", xs)], [("rstd", xs)], scale=-0.5)
                    stt("dve", xt[xs], xt[xs], st8[:, 12 + xs:13 + xs], A_bc, ALU.mult, ALU.mult,
                        [("xt", xs), ("rstd", xs), ("A_bc", 0), ("A_bc", 1)], [("xt", xs)])
                    tt("pool", hb[xs], xt[xs], SH_bc, ALU.add, [("xt", xs), ("SH_bc", 0), ("SH_bc", 1)], [("hb", xs)])

            def Htranspose(n):
                sl = n % 3
                for ti in range(4):
                    T = 4 * n + ti
                    xs = T % 4
                    pb_ = 0 if ti % 2 == 0 else 5
                    pt = bkb(pb_).rearrange("p (k t) -> p k t", k=8)
                    for kc in range(8):
                        tr(pt[:, kc, :], hb[xs][:, kc * 128:(kc + 1) * 128], [("hb", xs)], [("bank", pb_)])
                    cp("act", hT3[sl][:, :, ti * 128:(ti + 1) * 128], pt, [("bank", pb_)], [("hT", sl, ti)])

            def tokmm(n, ti):
                sl = n % 3
                lh = lambda kc: hT3[sl][:, kc, ti * 128:(ti + 1) * 128]
                for kc in range(8):
                    mm(bk(1), lh(kc), win3[:, kc, 0:512], kc == 0, kc == 7, [("hT", sl, ti)] + win_keys, [("bank", 1)])
                for kc in range(8):
                    mm(bk(2), lh(kc), win3[:, kc, O_CQ:O_CQ + 512], kc == 0, kc == 7, [("hT", sl, ti)] + win_keys, [("bank", 2)])
                for kc in range(8):
                    mm(bk(3)[:, 0:256], lh(kc), win3[:, kc, O_CV:O_CV + 256], kc == 0, kc == 7, [("hT", sl, ti)] + win_keys, [("bank", 3)])

            def post_grp(g_, T, ps_bank, nh, hd, gn_c, gn_s, rope_cos, rope_sin):
                p_ = T % 2
                width = GW[g_]
                rt = ropet[p_]
                kb = ("bank", ps_bank)
                bS, bQ, sv = bufS[g_][p_], bufQ[g_][p_], stat[g_][p_]
                kS, kQ, kst = ("bufS", g_, p_), ("bufQ", g_, p_), ("stat", g_, p_)
                psv = bk(ps_bank)[:, 0:width]
                v3 = lambda a: a.rearrange("p (h d) -> p h d", d=hd)
                cos_b = cust(rt, rope_cos, [[0, nh], [1, hd]])
                sin_b = cust(rt, rope_sin, [[0, nh], [1, hd]])
                tt("pool", v3(tabC[g_]), cos_b, v3(GN_bc[:, gn_c:gn_c + width]), ALU.mult, [("ropet", p_), "GN_bc"], [("tabC", g_)])
                tt("pool", v3(tabS[g_]), sin_b, v3(GN_bc[:, gn_s:gn_s + width]), ALU.mult, [("ropet", p_), "GN_bc"], [("tabS", g_)])
                yield
                act(bS, psv, AF.Square, [kb], [kS, (kS, 0), (kS, 1)])
                yield
                red(sv[:, 0:nh], v3(bS), [kS], [kst])
                ts("dve", sv[:, 0:nh], sv[:, 0:nh], 1.0 / hd, EPS, ALU.mult, ALU.add, [kst], [kst])
                yield
                act(sv[:, 16:16 + nh], sv[:, 0:nh], AF.Ln, [kst], [kst])
                act(sv[:, 32:32 + nh], sv[:, 16:16 + nh], AF.Exp, [kst], [kst], scale=-0.5)
                yield
                rs_b = cust(sv, 32, [[1, nh], [0, hd]])
                tt("dve", v3(bQ), v3(psv), rs_b, ALU.mult, [kb, kst], [kQ])
                yield
                hw = 16
                nb = hd // 32
                q5 = bQ.rearrange("p (g two w) -> p g two w", two=2, w=hw)
                o5 = bS.rearrange("p (g two w) -> p g two w", two=2, w=hw)
                t5 = tabS[g_].rearrange("p (g two w) -> p g two w", two=2, w=hw)
                for half in range(2):
                    tt("pool" if half else "dve", o5[:, :, half, :], q5[:, :, 1 - half, :], t5[:, :, half, :], ALU.mult,
                       [kQ, ("tabS", g_)], [(kS, half)])
                yield
                tt("dve", bQ, bQ, tabC[g_], ALU.mult, [kQ, ("tabC", g_)], [kQ])
                tt("dve", qkb[g_][p_], bQ, bS, ALU.add, [kQ, (kS, 0), (kS, 1)], [("qkb", g_, p_)])

            def post_tile(n, ti):
                T = 4 * n + ti
                cs = n % 2
                va = vAs[cs].rearrange("p (t h c) -> p t h c", t=4, h=2)
                vc = vCs[cs].rearrange("p (t h c) -> p t h c", t=4, h=4)
                P.dma("sp", ropet[T % 2], rope_d[T * 128:(T + 1) * 128, :], writes=[("ropet", T % 2)])
                gens = [post_grp("A", T, 1, 6, 64, 0, 896, 0, 64), post_grp("C", T, 2, 16, 32, 384, 1280, 128, 160)]
                step = 0
                while gens:
                    for g__ in list(gens):
                        try:
                            next(g__)
                        except StopIteration:
                            gens.remove(g__)
                    step += 1
                    if step == 2:
                        cp("act", va[:, ti, :, 0:64], bk(1)[:, 384:512].rearrange("p (h d) -> p h d", h=2), [("bank", 1)], [("vAs", cs, ti)])
                        cp("act", vc[:, ti, :, 0:64], bk(3)[:, 0:256].rearrange("p (h d) -> p h d", h=4), [("bank", 3)], [("vCs", cs, ti)])

            def qk_transposes(n, ti):
                T = 4 * n + ti
                cs = n % 2
                p_ = T % 2
                qat = qkAT[cs].rearrange("p (b t) -> p b t", b=3)
                qct = qkCT[cs].rearrange("p (b t) -> p b t", b=4)
                ptb = bkb(4).rearrange("p (b t) -> p b t", b=8)
                for b3 in range(3):
                    tr(ptb[:, b3, :], qkb["A"][p_][:, b3 * 128:(b3 + 1) * 128], [("qkb", "A", p_)], [("bank", 4)])
                cp("act", qat[:, :, ti * 128:(ti + 1) * 128], ptb[:, 0:3, :], [("bank", 4)], [("qkAT", cs, ti)])
                ptc = bkb(5).rearrange("p (b t) -> p b t", b=8)
                for b4 in range(4):
                    tr(ptc[:, b4, :], qkb["C"][p_][:, b4 * 128:(b4 + 1) * 128], [("qkb", "C", p_)], [("bank", 5)])
                cp("act", qct[:, :, ti * 128:(ti + 1) * 128], ptc[:, 0:4, :], [("bank", 5)], [("qkCT", cs, ti)])

            def tok_stores(n):
                cs = n % 2
                qat = qkAT[cs].rearrange("p (b t) -> p b t", b=3)
                qct = qkCT[cs].rearrange("p (b t) -> p b t", b=4)
                tk = lambda nm: [(nm, cs, ti) for ti in range(4)]
                c0, c1 = n * 512, (n + 1) * 512
                P.dma("sp", qTA_d[0:128, c0:c1], qat[:, 0, :], reads=tk("qkAT"), writes=[uq("qTA_d")])
                P.dma("sp", qTA_d[128:256, c0:c1], qat[:, 1, :], reads=tk("qkAT"), writes=[uq("qTA_d")])
                for kv in range(2):
                    for u in range(2):
                        P.dma("sp", kTA_d[kv * 128 + u * 64:kv * 128 + (u + 1) * 64, c0:c1], qat[kv * 64:(kv + 1) * 64, 2, :],
                              reads=tk("qkAT"), writes=[uq("kTA_d")])
                P.dma("sp", qTC_d[0:128, c0:c1], qct[:, 0, :], reads=tk("qkCT"), writes=[uq("qTC_d")])
                P.dma("sp", qTC_d[128:256, c0:c1], qct[:, 1, :], reads=tk("qkCT"), writes=[uq("qTC_d")])
                P.dma("sp", kTC_d[0:128, c0:c1], qct[:, 2, :], reads=tk("qkCT"), writes=[uq("kTC_d")])
                P.dma("sp", kTC_d[128:256, c0:c1], qct[:, 3, :], reads=tk("qkCT"), writes=[uq("kTC_d")])
                P.dma("sp", vA_d[c0:c1, :].rearrange("(t p) c -> p t c", p=128), vAs[cs].rearrange("p (t c) -> p t c", t=4),
                      reads=tk("vAs") + [("vAs1", cs)], writes=[uq("vA_d")])
                P.dma("sp", vC_d[c0:c1, :].rearrange("(t p) c -> p t c", p=128), vCs[cs].rearrange("p (t c) -> p t c", t=4),
                      reads=tk("vCs") + [("vCs1", cs)], writes=[uq("vC_d")])

            fb = [0]

            def feat_mm(n, col):
                sl = n % 3
                b = 6 + (fb[0] % 2)
                fb[0] += 1
                hk = [("hT", sl, ti) for ti in range(4)]
                for kc in range(8):
                    mm(bk(b), win3[:, kc, col:col + 128], hT3[sl][:, kc, :], kc == 0, kc == 7, hk + win_keys, [("bank", b)])
                return b

            def halo_mm(n, col, slot):
                res = []
                for side in range(2):
                    nn = n - 1 if side == 0 else n + 1
                    if nn < 0 or nn >= NCH:
                        res.append(None)
                        continue
                    sl = nn % 3
                    tcol = 511 if side == 0 else 0
                    ti = 3 if side == 0 else 0
                    o = bk(5)[:, 256 + slot * 2 + side: 256 + slot * 2 + side + 1]
                    for kc in range(8):
                        mm(o, win3[:, kc, col:col + 128], hT3[sl][:, kc, tcol:tcol + 1], kc == 0, kc == 7,
                           [("hT", sl, ti)] + win_keys, [("bank", 5)], skip_group_check=True)
                    res.append(o)
                return res

            def stage_out(dram_ap, src_fn, q="pool"):
                i = fst_i[0] % 4
                fst_i[0] += 1
                src_fn(fst[i], ("fst", i))
                P.dma(q, dram_ap, fst[i], reads=[("fst", i)], writes=[uq("scr_out")])

            def load_pre(n, col, pslot, hslot):
                b = feat_mm(n, col)
                hal = halo_mm(n, col, hslot)
                pk = ("pre", pslot)
                cp("act", pre[pslot][:, 1:513], bk(b), [("bank", b)], [pk])
                for side in range(2):
                    c = 0 if side == 0 else 513
                    if hal[side] is None:
                        mset("dve", pre[pslot][:, c:c + 1], 0.0, [pk], [pk])
                    else:
                        cp("act", pre[pslot][:, c:c + 1], hal[side], [("bank", 5), pk], [pk])
                return pk

            def conv3(dst, pslot, wt, wi, bias, reads, wkey):
                p_ = pre[pslot]
                if bias is not None:
                    ts("dve", dst, p_[:, 0:512], wt[:, wi * 3:wi * 3 + 1], bias, ALU.mult, ALU.add, reads, [wkey])
                else:
                    ts("dve", dst, p_[:, 0:512], wt[:, wi * 3:wi * 3 + 1], None, ALU.mult, ALU.bypass, reads, [wkey])
                stt("dve", dst, p_[:, 1:513], wt[:, wi * 3 + 1:wi * 3 + 2], dst, ALU.mult, ALU.add, reads + [wkey], [wkey])
                stt("dve", dst, p_[:, 2:514], wt[:, wi * 3 + 2:wi * 3 + 3], dst, ALU.mult, ALU.add, reads + [wkey], [wkey])

            def feat_gates(n):
                c0, c1 = n * 512, (n + 1) * 512
                for (col, dd) in ((O_AG, sgA_d), (O_CG, sgC_d)):
                    for hh in range(2):
                        b = feat_mm(n, col + hh * 128)
                        stage_out(dd[hh * 128:(hh + 1) * 128, c0:c1],
                                  lambda dst, key, b=b: act(dst, bk(b), AF.Silu, [("bank", b)], [key]), q="sp")

            def feat_B(n, hh):
                c0, c1 = n * 512, (n + 1) * 512
                k1 = load_pre(n, O_BP + (2 + hh) * 128, 0, 0)
                conv3(cva, 0, hycw, 2 + hh, hycb[:, 2 + hh:3 + hh], [k1, "hycw", "hycb"], "cva")
                k2 = load_pre(n, O_BP + (4 + hh) * 128, 1, 1)
                conv3(cvb, 1, hycw, 4 + hh, hycb[:, 4 + hh:5 + hh], [k2, "hycw", "hycb"], "cvb")
                stage_out(zT_d[hh * 128:(hh + 1) * 128, c0:c1],
                          lambda dst, key: tt("pool", dst, cva, cvb, ALU.mult, ["cva", "cvb"], [key]))
                k0 = load_pre(n, O_BP + hh * 128, 2, 2)
                conv3(cvc, 2, hycw, hh, hycb[:, hh:hh + 1], [k0, "hycw", "hycb"], "cvc")
                b = feat_mm(n, O_BG + hh * 128)
                act(sgt, bk(b), AF.Silu, [("bank", b)], ["sgt"])
                stage_out(xgB_d[hh * 128:(hh + 1) * 128, c0:c1],
                          lambda dst, key: tt("pool", dst, cvc, sgt, ALU.mult, ["cvc", "sgt"], [key]))

            def feat_D(n):
                c0, c1 = n * 512, (n + 1) * 512
                for hh in range(2):
                    k1 = load_pre(n, O_DP + (2 + hh) * 128, 0, 3)
                    k2 = load_pre(n, O_DP + (4 + hh) * 128, 1, 4)
                    tt("pool", pre[0], pre[0], pre[1], ALU.mult, [k1, k2], [k1])
                    conv3(cva, 0, sccw, hh, None, [k1, "sccw"], "cva")
                    b = feat_mm(n, O_DP + hh * 128)
                    tt("dve", cvb, cva, bk(b), ALU.mult, ["cva", ("bank", b)], ["cvb"])
                    b = feat_mm(n, O_DG + hh * 128)
                    act(sgt, bk(b), AF.Silu, [("bank", b)], ["sgt"])
                    stage_out(yT_d[768 + hh * 128:768 + (hh + 1) * 128, c0:c1],
                              lambda dst, key: tt("pool", dst, cvb, sgt, ALU.mult, ["cvb", "sgt"], [key]))

            Hcompute(0)
            Htranspose(0)
            if NCH > 1:
                Hcompute(1)
            dq = []

            stq = []

            def run_dq():
                for e_ in stq:
                    e_[0] -= 1
                while stq and stq[0][0] <= 0:
                    tok_stores(stq.pop(0)[1])
                if dq:
                    n_, ti_ = dq.pop(0)
                    qk_transposes(n_, ti_)
                    if ti_ == 3:
                        stq.append([2, n_])

            for n in range(NCH):
                for ti in range(4):
                    tokmm(n, ti)
                    post_tile(n, ti)
                    if ti == 0:
                        if n + 1 < NCH:
                            Htranspose(n + 1)
                        if n + 2 < NCH:
                            Hcompute(n + 2)
                        feat_gates(n)
                    elif ti == 1:
                        feat_B(n, 0)
                    elif ti == 2:
                        feat_B(n, 1)
                    else:
                        feat_D(n)
                    run_dq()
                    dq.append((n, ti))
            while dq:
                run_dq()
            while stq:
                tok_stores(stq.pop(0)[1])

        if "A" in stages:
            P.barrier()
            ar.off = ppos
            ztok = ar.alloc([256 * NT], BF16)
            ztok3 = ztok.rearrange("p (c s) -> p c s", c=256)
            ypr = ar.alloc([NT * 256], BF16)
            ypr3 = ypr.rearrange("p (t c) -> p t c", t=NT)
            NTB = 3
            tbuf = [ar.alloc([W], BF16) for _ in range(NTB)]
            xgc = [ar.alloc([512], BF16) for _ in range(2)]
            zcc = [ar.alloc([512], BF16) for _ in range(2)]
            et = ar.alloc([512], F32)
            ystB = [ar.alloc([512], BF16) for _ in range(2)]
            B_base = ar.off
            zc = [ar.alloc([2 * 512], BF16) for _ in range(2)]
            for n in range(NCH):
                zt = zc[n % 2].rearrange("p (h t) -> p h t", h=2)
                for hh in range(2):
                    P.dma("sp", zt[:, hh, :], zT_d[hh * 128:(hh + 1) * 128, n * 512:(n + 1) * 512], reads=["scr_out"], writes=[("zc", n % 2, hh)])
                b = n % 2
                pz = bkb(b).rearrange("p (t h c) -> p t h c", t=4, h=2)
                for ti in range(4):
                    for hh in range(2):
                        tr(pz[:, ti, hh, :], zt[:, hh, ti * 128:(ti + 1) * 128], [("zc", n % 2, hh)], [("bank", b)])
                dst = AP(ztok.tensor, ztok.offset + 4 * n, [list(ztok.ap[0]), [1, 4], [128 * NT, 2], [NT, 128]])
                cp("dve", dst, pz, [("bank", b)], [("ztok", n)])
            zkeys = [("ztok", n) for n in range(NCH)]
            Ds = [0] + [d for k in range(1, NT) for d in (k, -k)]
            NG = 256 // CPB

            NSB = 12 * NCH * NT
            BSKIP = 3
            BCH = max(1, -(-(256 * (len(Ds) + BSKIP * 0)) * BSKIP // max(1, NSB - 64)) + 1)

            def B_gen():
                for g in range(NG):
                    b = 7
                    py = bk(b)[:, 0:CPB * NT].rearrange("p (c t) -> p c t", c=CPB)
                    for cl in range(CPB):
                        c = g * CPB + cl
                        slot = c % NTB
                        P.dma("sp", tbuf[slot], AP(gflt_d, c * L2, [[1, 128], [1, W]]), reads=["gflt"], writes=[("tbuf", slot)])
                        for di, D in enumerate(Ds):
                            tlo, thi = max(0, D), min(NT - 1, NT - 1 + D)
                            n0 = (S - 128) - 128 * D
                            mm(py[:, cl, tlo:thi + 1], tbuf[slot][:, n0:n0 + 128], ztok3[:, c, tlo - D:thi - D + 1],
                               di == 0, di == len(Ds) - 1, [("tbuf", slot)] + zkeys, [("bank", b)], skip_group_check=True)
                            if di % BCH == BCH - 1:
                                yield
                        if cl == CPB - 1:
                            dst = AP(ypr.tensor, ypr.offset + g * CPB, [list(ypr.ap[0]), [1, CPB], [256, NT]])
                            cp("dve", dst, py, [("bank", b)], [("ypr", g)])
                        yield

            bgen = B_gen()
            sbc = [0]
            bcall = [0]

            def bstep():
                bcall[0] += 1
                if bcall[0] % BSKIP:
                    return
                try:
                    next(bgen)
                except StopIteration:
                    pass

            ar.off = B_base
            AUX = 6
            kdup = ar.alloc([S], BF16)
            qpair = ar.alloc([S], BF16)
            vsb = ar.alloc([NT * 130], BF16)
            vsb4 = vsb.rearrange("p (t h c) -> p t h c", t=NT, h=2)
            pTd = [ar.alloc([1024], BF16) for _ in range(2)]
            osbA = [ar.alloc([512], F32) for _ in range(2)]
            rden = ar.alloc([512], F32)
            bden = ar.alloc([512], F32)
            otmp = ar.alloc([512], F32)
            sgl = [[ar.alloc([512], BF16) for _ in range(2)] for _ in range(2)]
            yst = [ar.alloc([512], BF16) for _ in range(2)]
            scale_a = 64 ** -0.5
            it = 0
            pend = []

            def flush():
                while pend:
                    pend.pop(0)()

            for q4 in range(NT // 4):
                P.dma("sp", vsb.rearrange("p (t c) -> p t c", t=NT)[:, q4 * 4:(q4 + 1) * 4, :],
                      vA_d[q4 * 512:(q4 + 1) * 512, :].rearrange("(t p) c -> p t c", p=128), reads=["vA_d"], writes=["vsb"])
            for g in range(2):
                P.dma("sp", kdup, kTA_d[g * 128:(g + 1) * 128, :], reads=["kTA_d"], writes=["kdup"])
                P.dma("sp", qpair, qTA_d[g * 128:(g + 1) * 128, :], reads=["qTA_d"], writes=["qpair"])
                for qi in range(NCH):
                    s2 = it % 2
                    for hh in range(2):
                        h_ = 2 * g + hh
                        P.dma("sp", sgl[s2][hh][0:64, :], sgA_d[h_ * 64:(h_ + 1) * 64, qi * 512:(qi + 1) * 512], reads=["scr_out"], writes=[("sgl", s2, hh)])

                    def qk(sb, qi=qi):
                        p_ = sb % 2
                        for hh in range(2):
                            b = 2 * p_ + hh
                            rows = slice(hh * 64, (hh + 1) * 64)
                            mm(bk(b), kdup[rows, sb * 128:(sb + 1) * 128], qpair[rows, qi * 512:(qi + 1) * 512], True, True,
                               ["kdup", "qpair"], [("bank", b)])

                    qk(0)
                    if NT > 1:
                        qk(1)
                    for sb in range(NT):
                        p_ = sb % 2
                        for hh in range(2):
                            act(pTd[p_][:, hh * 512:(hh + 1) * 512], bk(2 * p_ + hh), AF.Exp, [("bank", 2 * p_ + hh)], [("pT", 2 * p_ + hh)], scale=scale_a)
                        if pend and sb % 4 == 1:
                            pend.pop(0)()
                        for hh in range(2):
                            if hh == 1 and sb + 2 < NT:
                                qk(sb + 2)
                            mm(bk(4 + hh)[0:65, :], vsb4[:, sb, g, :], pTd[p_][:, hh * 512:(hh + 1) * 512], sb == 0, sb == NT - 1,
                               ["vsb", ("pT", 2 * p_ + hh)], [("bank", 4 + hh)])
                            bstep()
                    flush()
                    for hh in range(2):
                        cp("dve", osbA[hh][0:65, :], bk(4 + hh)[0:65, :], [("bank", 4 + hh)], [("osbA", hh)])
                    for hh in range(2):
                        h = 2 * g + hh

                        def epi1(hh=hh):
                            rcp(rden[64:65, :], osbA[hh][64:65, :], [("osbA", hh)], ["rden"])
                            mm(bk(AUX)[0:64, :], ones_f[64:65, 0:64], rden[64:65, :], True, True, ["ones_f", "rden"], [("bank", AUX)])

                        def epi2(hh=hh, h=h, qi=qi, s2=s2):
                            cp("dve", bden[0:64, :], bk(AUX)[0:64, :], [("bank", AUX)], ["bden"])
                            tt("dve", otmp[0:64, :], osbA[hh][0:64, :], bden[0:64, :], ALU.mult, [("osbA", hh), "bden"], ["otmp"])
                            tt("pool", yst[hh][0:64, :], otmp[0:64, :], sgl[s2][hh][0:64, :], ALU.mult, ["otmp", ("sgl", s2, hh)], [("yst", hh)])
                            P.dma("pool", yT_d[h * 64:(h + 1) * 64, qi * 512:(qi + 1) * 512], yst[hh][0:64, :], reads=[("yst", hh)], writes=[uq("yT_A")])
                        pend.extend([epi1, epi2])
                    it += 1
            flush()

            P.barrier()
            ar.off = B_base
            kzp = [ar.alloc([S], BF16) for _ in range(2)]
            qblk = ar.alloc([S], BF16)
            vsbc = ar.alloc([NT * 260], BF16)
            vsbc4 = vsbc.rearrange("p (t h c) -> p t h c", t=NT, h=4)
            pTd = [ar.alloc([1024], BF16) for _ in range(2)]
            osb = [[ar.alloc([512], F32) for _ in range(2)] for _ in range(2)]
            rden = [ar.alloc([512], F32) for _ in range(2)]
            bden = [ar.alloc([512], F32) for _ in range(2)]
            o1 = ar.alloc([512], F32)
            o2 = ar.alloc([512], F32)
            dsq = ar.alloc([512], F32)
            rs = ar.alloc([512], F32)
            rtmp = ar.alloc([512], F32)
            sgl = [[ar.alloc([512], BF16) for _ in range(2)] for _ in range(2)]
            yst = [ar.alloc([512], BF16) for _ in range(2)]
            scale_c = 32 ** -0.5
            AUX = 6
            it = 0
            pend = []

            def flush():
                while pend:
                    pend.pop(0)()

            for q4 in range(NT // 4):
                P.dma("sp", vsbc.rearrange("p (t c) -> p t c", t=NT)[:, q4 * 4:(q4 + 1) * 4, :],
                      vC_d[q4 * 512:(q4 + 1) * 512, :].rearrange("(t p) c -> p t c", p=128), reads=["vC_d"], writes=["vsbc"])
            for hb in range(2):
                P.dma("sp", qblk, qTC_d[hb * 128:(hb + 1) * 128, :], reads=["qTC_d"], writes=["qblk"])
                for c in range(2):
                    mset("pool", kzp[c], 0.0, [], [("kzp", c)])
                    for hh in range(2):
                        r0 = hh * 64 + c * 32
                        P.dma("sp", kzp[c][r0:r0 + 32, :], kTC_d[hb * 128 + r0:hb * 128 + r0 + 32, :],
                              reads=["kTC_d", ("kzp", c)], writes=[("kzp", c)])
                for qi in range(NCH):
                    s2 = it % 2
                    for hh in range(2):
                        h_ = 2 * hb + hh
                        P.dma("sp", sgl[s2][hh][0:64, :], sgC_d[h_ * 64:(h_ + 1) * 64, qi * 512:(qi + 1) * 512], reads=["scr_out"], writes=[("sgl", s2, hh)])
                    for c in range(2):

                        def qk(sb, c=c, qi=qi):
                            p_ = sb % 2
                            for hh in range(2):
                                b = 2 * p_ + hh
                                rows = slice(hh * 64, (hh + 1) * 64)
                                mm(bk(b), kzp[c][rows, sb * 128:(sb + 1) * 128], qblk[rows, qi * 512:(qi + 1) * 512], True, True,
                                   [("kzp", c), "qblk"], [("bank", b)])

                        qk(0)
                        if NT > 1:
                            qk(1)
                        for sb in range(NT):
                            p_ = sb % 2
                            for hh in range(2):
                                act(pTd[p_][:, hh * 512:(hh + 1) * 512], bk(2 * p_ + hh), AF.Exp, [("bank", 2 * p_ + hh)], [("pT", 2 * p_ + hh)], scale=scale_c)
                            if pend and sb % 3 == 1:
                                pend.pop(0)()
                            for hh in range(2):
                                if hh == 1 and sb + 2 < NT:
                                    qk(sb + 2)
                                mm(bk(4 + hh)[0:65, :], vsbc4[:, sb, 2 * hb + hh, :], pTd[p_][:, hh * 512:(hh + 1) * 512], sb == 0, sb == NT - 1,
                                   ["vsbc", ("pT", 2 * p_ + hh)], [("bank", 4 + hh)])
                                bstep()
                        if c == 0:
                            flush()
                        for hh in range(2):
                            cp("dve", osb[hh][c][0:65, :], bk(4 + hh)[0:65, :], [("bank", 4 + hh)], [("osb", hh, c)])

                    for hh in range(2):
                        h = 2 * hb + hh

                        def e1(hh=hh):
                            rcp(rden[0][64:65, :], osb[hh][0][64:65, :], [("osb", hh, 0)], [("rden", 0)])
                            rcp(rden[1][64:65, :], osb[hh][1][64:65, :], [("osb", hh, 1)], [("rden", 1)])
                            mm(bk(AUX)[0:64, :], ones_f[64:65, 0:64], rden[0][64:65, :], True, True, ["ones_f", ("rden", 0)], [("bank", AUX)])

                        def e2():
                            cp("dve", bden[0][0:64, :], bk(AUX)[0:64, :], [("bank", AUX)], [("bden", 0)])
                            mm(bk(AUX)[0:64, :], ones_f[64:65, 0:64], rden[1][64:65, :], True, True, ["ones_f", ("rden", 1)], [("bank", AUX)])

                        def e3(hh=hh):
                            cp("dve", bden[1][0:64, :], bk(AUX)[0:64, :], [("bank", AUX)], [("bden", 1)])
                            tt("dve", o1[0:64, :], osb[hh][0][0:64, :], bden[0][0:64, :], ALU.mult, [("osb", hh, 0), ("bden", 0)], ["o1"])
                            tt("dve", o2[0:64, :], osb[hh][1][0:64, :], bden[1][0:64, :], ALU.mult, [("osb", hh, 1), ("bden", 1)], ["o2"])
                            stt("dve", o1[0:64, :], o2[0:64, :], neglam[0:64, :], o1[0:64, :], ALU.mult, ALU.add, ["o1", "o2", "neglam"], ["o1"])
                            tt("pool", dsq[0:64, :], o1[0:64, :], o1[0:64, :], ALU.mult, ["o1"], ["dsq"])

                        def e4():
                            mm(bk(AUX)[0:64, :], ones_f[0:64, 0:64], dsq[0:64, :], True, True, ["ones_f", "dsq"], [("bank", AUX)])

                        def e5(h=h, hh=hh, qi=qi, s2=s2):
                            ts("dve", rtmp[0:64, :], bk(AUX)[0:64, :], 1.0 / 64, EPS, ALU.mult, ALU.add, [("bank", AUX)], ["rtmp"])
                            act(rtmp[0:64, :], rtmp[0:64, :], AF.Ln, ["rtmp"], ["rtmp"])
                            act(rs[0:64, :], rtmp[0:64, :], AF.Exp, ["rtmp"], ["rs"], scale=-0.5)
                            stt("dve", o2[0:64, :], o1[0:64, :], subg[0:64, :], rs[0:64, :], ALU.mult, ALU.mult, ["o1", "subg", "rs"], ["o2"])
                            tt("pool", yst[hh][0:64, :], o2[0:64, :], sgl[s2][hh][0:64, :], ALU.mult, ["o2", ("sgl", s2, hh)], [("yst", hh)])
                            P.dma("pool", yT_d[512 + h * 64:512 + (h + 1) * 64, qi * 512:(qi + 1) * 512], yst[hh][0:64, :], reads=[("yst", hh)], writes=[uq("yT_C")])
                        pend.extend([e1, e2, e3, e4, e5])
                    it += 1
            flush()

            for _ in bgen:
                pass
            ykeys = [("ypr", g) for g in range(NG)]
            it = 0
            for n in range(NCH):
                for hh in range(2):
                    b = 4 + (it % 2)
                    for ti in range(4):
                        mm(bk(b)[:, ti * 128:(ti + 1) * 128], ypr3[:, 4 * n + ti, hh * 128:(hh + 1) * 128], jmat, True, True,
                           ykeys + ["jmat"], [("bank", b)], skip_group_check=True)
                    P.dma("sp", zcc[it % 2], zT_d[hh * 128:(hh + 1) * 128, n * 512:(n + 1) * 512], reads=["scr_out"], writes=[("zcc", it % 2)])
                    P.dma("sp", xgc[it % 2], xgB_d[hh * 128:(hh + 1) * 128, n * 512:(n + 1) * 512], reads=["scr_out"], writes=[("xgc", it % 2)])
                    stt("dve", et, zcc[it % 2], hybias[:, hh:hh + 1], bk(b), ALU.mult, ALU.add, [("zcc", it % 2), "hybias", ("bank", b)], ["et"])
                    tt("dve", ystB[it % 2], et, xgc[it % 2], ALU.mult, ["et", ("xgc", it % 2)], [("ystB", it % 2)])
                    P.dma("pool", yT_d[256 + hh * 128:256 + (hh + 1) * 128, n * 512:(n + 1) * 512], ystB[it % 2], reads=[("ystB", it % 2)], writes=[uq("yT_B")])
                    it += 1

        if "O" in stages:
            P.barrier()
            ar.off = ppos
            wo = ar.alloc([8 * 1024], BF16)
            wo3 = wo.rearrange("p (k c) -> p k c", k=8)
            yc = [ar.alloc([8 * 512], BF16) for _ in range(2)]
            xt = [ar.alloc([1024], F32) for _ in range(3)]
            ot = [ar.alloc([1024], F32) for _ in range(2)]
            rt_ = [ar.alloc([1024], F32) for _ in range(2)]
            for kc in range(8):
                P.dma("pool", wo3[:, kc, :], wout_d[l, kc * 128:(kc + 1) * 128, :], writes=[("wo", kc)])
            wok = [("wo", kc) for kc in range(8)]
            for n in range(NCH):
                y3 = yc[n % 2].rearrange("p (k t) -> p k t", k=8)
                P.dma("sp", y3, yT_d[:, n * 512:(n + 1) * 512].rearrange("(k p) t -> p k t", p=128),
                      reads=["yT_A", "yT_B", "yT_C", "scr_out"], writes=[("yc", n % 2)])
                for ti in range(4):
                    T = 4 * n + ti
                    xs = T % 3
                    P.dma("sp", xt[xs], xsrc[T * 128:(T + 1) * 128, :], reads=[("xout", l - 1, T)], writes=[("xt", xs)])
                    for h in range(2):
                        b = (2 * T + h) % 4
                        for kc in range(8):
                            mm(bk(b), y3[:, kc, ti * 128:(ti + 1) * 128], wo3[:, kc, h * 512:(h + 1) * 512], kc == 0, kc == 7,
                               [("yc", n % 2)] + wok, [("bank", b)])
                        tt("dve", ot[T % 2][:, h * 512:(h + 1) * 512], bk(b), G_bc[:, h * 512:(h + 1) * 512], ALU.mult,
                           [("bank", b), ("G_bc", h)], [("ot", T % 2, h)])
                    tt("pool" if T % 2 else "dve", rt_[T % 2], ot[T % 2], xt[xs], ALU.add, [("ot", T % 2, 0), ("ot", T % 2, 1), ("xt", xs)], [("rt", T % 2)])
                    P.dma("pool", xdst[T * 128:(T + 1) * 128, :], rt_[T % 2], reads=[("rt", T % 2)], writes=[("xout", l, T)])

    P.barrier()
    P.op("sp", lambda e: e.nop(), [], [])
    P.emit()
    return nc


def host_tables(S):
    f32 = np.float32
    L = S
    t = np.arange(S)
    row = (t // 64).astype(f32)
    col = (t % 64).astype(f32)
    inv = (np.float32(10000.0) ** (-np.arange(0, 32, 2, dtype=f32) / np.float32(32))).astype(f32)

    def cs(pos):
        ang = pos.astype(f32)[:, None] * inv[None, :]
        return np.cos(ang).astype(f32), np.sin(ang).astype(f32)
    cr, sr = cs(row)
    cc, sc = cs(col)
    c1, s1 = cs(t.astype(f32))
    rope = np.concatenate([cr, cr, cc, cc, -sr, sr, -sc, sc, c1, c1, -s1, s1], axis=1).astype(f32)
    tt_ = np.linspace(0.0, 1.0, L, dtype=f32)[:, None]
    w = (np.float32(2.0 * math.pi) * np.arange(L, dtype=f32)[:, None] / np.float32(L)).astype(f32)
    f = np.linspace(1e-4, 15, 16, dtype=f32)[None, :]
    emb = np.concatenate([tt_, np.cos(f * w).astype(f32), -np.sin(f * w).astype(f32)], axis=-1).astype(f32)
    d = L - 1 - np.arange(2 * L)
    pos = np.abs(d)
    pos[pos >= L] = 0
    emb_tab = np.ascontiguousarray(emb[pos].T)
    tpos = tt_[pos, 0][None, :].astype(f32)
    max_decay = math.log(1e-2) / 0.3
    min_decay = math.log(1e-2) / 1.5
    deltas = np.linspace(min_decay, max_decay, 256, dtype=f32)
    negdelta = np.ascontiguousarray((-np.abs(deltas)).reshape(2, 128).T).astype(f32)
    eye = np.eye(128, dtype=f32)
    jm = np.ascontiguousarray(eye[::-1])
    return dict(rope_tab=rope, emb_tab=emb_tab, tpos_tab=tpos, negdelta=negdelta, eye_tab=eye, jmat_tab=jm)


def host_layout(inp, b, NL):
    f32 = np.float32
    g = lambda k: np.asarray(inp[k], dtype=f32)
    m = {}
    m["x"] = np.ascontiguousarray(g("x")[b])
    m["cT"] = np.ascontiguousarray(g("c")[b].reshape(8, 128).T)
    m["norm_g"] = g("norm_g")[:NL, None, :]
    m["w_ada"] = g("w_ada")[:NL]
    m["b_ada"] = g("b_ada")[:NL, None, :]
    m["w_in"] = g("w_in")[:NL]
    m["w_out"] = g("w_out")[:NL]
    def swp(a, hd):
        a4 = a.reshape(a.shape[0], hd // 32, 2, 16)
        return a4[:, :, ::-1, :].reshape(a.shape[0], hd)
    aq, ak, cq, ck = g("a_qn")[:NL], g("a_kn")[:NL], g("c_qn")[:NL], g("c_kn")[:NL]
    m["gnrow"] = np.concatenate([np.tile(aq, (1, 4)), np.tile(ak, (1, 2)), np.tile(cq, (1, 8)), np.tile(ck, (1, 8)),
                                 np.tile(swp(aq, 64), (1, 4)), np.tile(swp(ak, 64), (1, 2)),
                                 np.tile(swp(cq, 32), (1, 8)), np.tile(swp(ck, 32), (1, 8))], axis=1)[:, None, :]
    m["hy_conv_wT"] = np.ascontiguousarray(g("hy_conv_w")[:NL].reshape(NL, 3, 6, 128).transpose(0, 3, 2, 1))
    m["hy_conv_bT"] = np.ascontiguousarray(g("hy_conv_b")[:NL].reshape(NL, 6, 128).transpose(0, 2, 1))
    m["hy_w1"] = g("hy_w1")[:NL]
    m["hy_b1T"] = g("hy_b1")[:NL, :, None]
    m["hy_freqT"] = g("hy_freq")[:NL, :, None]
    m["hy_w2"] = g("hy_w2")[:NL]
    m["hy_b2T"] = g("hy_b2")[:NL, :, None]
    m["hy_w3"] = g("hy_w3")[:NL]
    m["hy_biasT"] = np.ascontiguousarray(g("hy_bias")[:NL].reshape(NL, 2, 128).transpose(0, 2, 1))
    m["lamrow"] = np.concatenate([g("lam_q1")[:NL], g("lam_k1")[:NL], g("lam_q2")[:NL], g("lam_k2")[:NL]], axis=1)[:, None, :]
    m["c_sublnT"] = g("c_subln")[:NL, :, None]
    m["sc_conv_wT"] = np.ascontiguousarray(g("sc_conv_w")[:NL].reshape(NL, 3, 2, 128).transpose(0, 3, 2, 1))
    return {k: np.ascontiguousarray(v, dtype=f32) for k, v in m.items()}


_CACHE = {}


def kernel(**inputs):
    x = np.asarray(inputs["x"])
    B, S, _ = x.shape
    NL = np.asarray(inputs["w_in"]).shape[0]
    key = (S, NL)
    if key not in _CACHE:
        _CACHE[key] = (build(S, NL), host_tables(S))
    nc, tabs = _CACHE[key]
    in_maps = []
    for b in range(B):
        m = host_layout(inputs, b, NL)
        m.update(tabs)
        in_maps.append(m)
    res = run_bass_kernel_spmd(nc, in_maps, core_ids=list(range(B)))
    return np.stack([np.asarray(r["out"]) for r in res.results], axis=0).astype(np.float32)
```
